# Optimizing a Trainium2 kernel written in Bass

```python
import jax, jax.numpy as jnp
from jax import lax
import numpy as np

D_MODEL = 1024
BATCH = 4
SEQ = 8192
DEPTH = 2

CTX_LEN = 256
GRID_W = 64
N_EVEN = (DEPTH + 1) // 2
N_ODD = DEPTH // 2
N_MOD = 6
EPS = 1e-6

MIX_WIDTH = D_MODEL
FOURIER_WIDTH = MIX_WIDTH // 2
FOURIER_GROUPS = 4
FOURIER_GROUP_DIM = FOURIER_WIDTH // FOURIER_GROUPS
GLA_HEADS = 4
GLA_VAL_WIDTH = MIX_WIDTH - FOURIER_WIDTH
GLA_DV = GLA_VAL_WIDTH // GLA_HEADS
GLA_DK = GLA_DV // 2
GLA_KEY_WIDTH = GLA_HEADS * GLA_DK
GLA_GATE_RANK = 16
GLA_GATE_TEMP = 16.0
GLA_CHUNK = 64
EVEN_SPLITS = [FOURIER_WIDTH,
               FOURIER_WIDTH + GLA_KEY_WIDTH,
               FOURIER_WIDTH + 2 * GLA_KEY_WIDTH,
               FOURIER_WIDTH + 2 * GLA_KEY_WIDTH + GLA_VAL_WIDTH,
               FOURIER_WIDTH + 2 * GLA_KEY_WIDTH + 2 * GLA_VAL_WIDTH]
EVEN_IN_WIDTH = EVEN_SPLITS[-1] + GLA_GATE_RANK

HEAD_DIM = 128
ATT_HEADS = D_MODEL // HEAD_DIM
ATT_KV_HEADS = 2
ATT_GROUP = ATT_HEADS // ATT_KV_HEADS
Q_WIDTH = ATT_HEADS * HEAD_DIM
QKV_WIDTH = (ATT_HEADS + 2 * ATT_KV_HEADS) * HEAD_DIM
Q_BLOCK = 128
ROPE_THETA = 10000.0
ROPE_AXIS_DIM = HEAD_DIM // 2

D_FF = ((8 * D_MODEL // 3 + 255) // 256) * 256
CONV_W = 3

kernel_name = 'hybrid_fourier_gla_gqa_dit_prefix'


def rms_norm(x, g):
    xf = x.astype(jnp.float32)
    y = xf * lax.rsqrt(jnp.mean(xf * xf, axis=-1, keepdims=True) + EPS)
    return (y * g.astype(jnp.float32)).astype(x.dtype)


def adaln(cv, w, b):
    m = jax.nn.silu(cv) @ w + b
    m = m.reshape(cv.shape[:-1] + (1, N_MOD, D_MODEL))
    return [m[..., k, :] for k in range(N_MOD)]


def dwconv_centred(x, w, b):
    L = x.shape[1]
    pad = CONV_W // 2
    xp = jnp.pad(x, ((0, 0), (pad, CONV_W - 1 - pad), (0, 0)))
    out = b
    for k in range(CONV_W):
        out = out + xp[:, k:k + L] * w[k]
    return out


def conv_ffn(h, w_up, w_conv, b_conv, w_down):
    u = dwconv_centred(h @ w_up, w_conv, b_conv)
    g, v = jnp.split(u, 2, axis=-1)
    return (jax.nn.silu(g) * v) @ w_down


def fourier_mix(u):
    B_, L, _ = u.shape
    ug = u.reshape(B_, L, FOURIER_GROUPS, FOURIER_GROUP_DIM).astype(jnp.float32)
    f = jnp.fft.fft2(ug, axes=(1, 3), norm='ortho').real
    return f.reshape(B_, L, FOURIER_WIDTH).astype(u.dtype)


def gla_scan(q, k, v, loga, s0):
    q, k, v = (t.astype(jnp.float32) for t in (q, k, v))
    B_, H, L, _ = q.shape
    n = L // GLA_CHUNK

    def to_chunks(t):
        return jnp.moveaxis(t.reshape(B_, H, n, GLA_CHUNK, t.shape[-1]), 2, 0)

    mask = jnp.tril(jnp.ones((GLA_CHUNK, GLA_CHUNK), bool))[:, :, None]

    def step(S, inp):
        qc, kc, vc, ac = inp
        b = jnp.cumsum(ac, axis=2)
        diff = b[:, :, :, None, :] - b[:, :, None, :, :]
        decay = jnp.exp(jnp.where(mask, diff, -jnp.inf))
        attn = jnp.einsum('bhid,bhjd,bhijd->bhij', qc, kc, decay)
        o = attn @ vc + jnp.einsum('bhid,bhde->bhie', qc * jnp.exp(b), S)
        b_last = b[:, :, -1:, :]
        S_new = (jnp.exp(b_last[:, :, 0, :, None]) * S
                 + jnp.einsum('bhjd,bhje->bhde', kc * jnp.exp(b_last - b), vc))
        return S_new, o

    s_fin, o = lax.scan(step, s0, (to_chunks(q), to_chunks(k), to_chunks(v), to_chunks(loga)))
    o = jnp.moveaxis(o, 0, 2).reshape(B_, H, L, v.shape[-1])
    return o, s_fin


def even_mix(h, s_f, s_b, w_in, w_gate, b_gate, g_gla, w_out):
    B_, L, _ = h.shape
    p = h @ w_in
    u_f, q, k, v, r, z = jnp.split(p, EVEN_SPLITS, axis=-1)

    def heads(t, d):
        return t.reshape(B_, L, GLA_HEADS, d).transpose(0, 2, 1, 3)

    q = heads(q, GLA_DK) * (GLA_DK ** -0.5)
    k = heads(k, GLA_DK)
    v = heads(v, GLA_DV)
    zf = z.astype(jnp.float32)
    loga = [heads(jax.nn.log_sigmoid(zf @ w_gate[d].astype(jnp.float32) + b_gate[d].astype(jnp.float32))
                  / GLA_GATE_TEMP, GLA_DK) for d in range(2)]

    def flip(t):
        return jnp.flip(t, axis=2)

    o_f, s_f = gla_scan(q, k, v, loga[0], s_f)
    o_b, s_b = gla_scan(flip(q), flip(k), flip(v), flip(loga[1]), s_b)
    o = o_f + flip(o_b)
    o = o * lax.rsqrt(jnp.mean(o * o, axis=-1, keepdims=True) + EPS)
    o = o.transpose(0, 2, 1, 3).reshape(B_, L, GLA_VAL_WIDTH) * g_gla.astype(jnp.float32)
    o = (o * jax.nn.silu(r.astype(jnp.float32))).astype(h.dtype)
    y = jnp.concatenate([fourier_mix(u_f), o], axis=-1) @ w_out
    return y, s_f, s_b


def axial_rope_tables(n_tokens):
    rows = n_tokens // GRID_W
    r, c = jnp.meshgrid(jnp.arange(rows), jnp.arange(GRID_W), indexing='ij')
    inv = ROPE_THETA ** (-jnp.arange(0, ROPE_AXIS_DIM, 2, dtype=jnp.float32) / ROPE_AXIS_DIM)
    ang = jnp.concatenate([r.reshape(-1, 1).astype(jnp.float32) * inv,
                           c.reshape(-1, 1).astype(jnp.float32) * inv], axis=-1)
    return jnp.cos(ang), jnp.sin(ang)


def apply_rope(t, cos, sin):
    tf = t.astype(jnp.float32)
    t1, t2 = jnp.split(tf, 2, axis=-1)
    return jnp.concatenate([t1 * cos - t2 * sin, t1 * sin + t2 * cos], axis=-1).astype(t.dtype)


def attn_q(h, w_q, g_q):
    B_, L, _ = h.shape
    q = rms_norm((h @ w_q).reshape(B_, L, ATT_HEADS, HEAD_DIM), g_q)
    return q.transpose(0, 2, 1, 3)


def attn_kv(h, w_kv, g_k):
    B_, L, _ = h.shape
    kv = (h @ w_kv).reshape(B_, L, 2, ATT_KV_HEADS, HEAD_DIM)
    k = rms_norm(kv[:, :, 0], g_k).transpose(0, 2, 1, 3)
    v = kv[:, :, 1].transpose(0, 2, 1, 3)
    return k, v


def grouped_attend(q, k, v):
    s = jnp.einsum('bkgqd,bksd->bkgqs', q, k).astype(jnp.float32) * (HEAD_DIM ** -0.5)
    p = jax.nn.softmax(s, axis=-1).astype(v.dtype)
    return jnp.einsum('bkgqs,bksd->bkgqd', p, v)


def setup_inputs(seed: int = 0) -> dict:
    key = jax.random.key(seed)
    ks = jax.random.split(key, 22)
    f32 = jnp.float32

    def nrm(k, shape, scale):
        return jax.random.normal(k, shape, f32) * scale

    def gain(k, shape):
        return 1.0 + 0.02 * jax.random.normal(k, shape, f32)

    return {
        'x': nrm(ks[0], (BATCH, SEQ, D_MODEL), 1.0),
        'c': nrm(ks[1], (BATCH, D_MODEL), 1.0),
        'ctx': nrm(ks[2], (BATCH, CTX_LEN, D_MODEL), 1.0),
        'c_ctx': nrm(ks[3], (D_MODEL,), 1.0),
        'w_mod': nrm(ks[4], (DEPTH, D_MODEL, N_MOD * D_MODEL), D_MODEL ** -0.5),
        'b_mod': nrm(ks[5], (DEPTH, N_MOD * D_MODEL), 0.02),
        'g_norm_mix': gain(ks[6], (DEPTH, D_MODEL)),
        'g_norm_ffn': gain(ks[7], (DEPTH, D_MODEL)),
        'g_norm_final': gain(ks[8], (D_MODEL,)),
        'w_even_in': nrm(ks[9], (N_EVEN, D_MODEL, EVEN_IN_WIDTH), D_MODEL ** -0.5),
        'w_gla_gate': nrm(ks[10], (N_EVEN, 2, GLA_GATE_RANK, GLA_KEY_WIDTH), GLA_GATE_RANK ** -0.5),
        'b_gla_gate': nrm(ks[11], (N_EVEN, 2, GLA_KEY_WIDTH), 0.1),
        'g_gla_out': gain(ks[12], (N_EVEN, GLA_VAL_WIDTH)),
        'w_even_out': nrm(ks[13], (N_EVEN, MIX_WIDTH, D_MODEL), MIX_WIDTH ** -0.5),
        'w_qkv': nrm(ks[14], (N_ODD, D_MODEL, QKV_WIDTH), D_MODEL ** -0.5),
        'g_q': gain(ks[15], (N_ODD, HEAD_DIM)),
        'g_k': gain(ks[16], (N_ODD, HEAD_DIM)),
        'w_att_out': nrm(ks[17], (N_ODD, Q_WIDTH, D_MODEL), Q_WIDTH ** -0.5),
        'w_ffn_up': nrm(ks[18], (DEPTH, D_MODEL, 2 * D_FF), D_MODEL ** -0.5),
        'w_ffn_conv': nrm(ks[19], (DEPTH, CONV_W, 2 * D_FF), CONV_W ** -0.5),
        'b_ffn_conv': nrm(ks[20], (DEPTH, 2 * D_FF), 0.02),
        'w_ffn_down': nrm(ks[21], (DEPTH, D_FF, D_MODEL), D_FF ** -0.5),
    }


def reference(x, c, ctx, c_ctx, w_mod, b_mod, g_norm_mix, g_norm_ffn, g_norm_final,
              w_even_in, w_gla_gate, b_gla_gate, g_gla_out, w_even_out,
              w_qkv, g_q, g_k, w_att_out,
              w_ffn_up, w_ffn_conv, b_ffn_conv, w_ffn_down):
    B_, S, _ = x.shape
    cos, sin = axial_rope_tables(S)
    h_lat, h_ctx = x, ctx
    for i in range(DEPTH):
        last = i == DEPTH - 1
        j = i // 2
        sh1, sc1, ga1, sh2, sc2, ga2 = adaln(c, w_mod[i], b_mod[i])
        csh1, csc1, cga1, csh2, csc2, cga2 = adaln(c_ctx, w_mod[i], b_mod[i])
        a_lat = rms_norm(h_lat, g_norm_mix[i]) * (1 + sc1) + sh1
        a_ctx = rms_norm(h_ctx, g_norm_mix[i]) * (1 + csc1) + csh1
        if i % 2 == 0:
            zero = jnp.zeros((B_, GLA_HEADS, GLA_DK, GLA_DV), jnp.float32)
            o_ctx, s_f, s_b = even_mix(a_ctx, zero, zero, w_even_in[j], w_gla_gate[j],
                                       b_gla_gate[j], g_gla_out[j], w_even_out[j])
            o_lat, _, _ = even_mix(a_lat, s_f, s_b, w_even_in[j], w_gla_gate[j],
                                   b_gla_gate[j], g_gla_out[j], w_even_out[j])
        else:
            w_q = w_qkv[j][:, :Q_WIDTH]
            w_kv = w_qkv[j][:, Q_WIDTH:]
            q_l = apply_rope(attn_q(a_lat, w_q, g_q[j]), cos, sin)
            k_l, v_l = attn_kv(a_lat, w_kv, g_k[j])
            k_l = apply_rope(k_l, cos, sin)
            k_c, v_c = attn_kv(a_ctx, w_kv, g_k[j])
            k_all = jnp.concatenate([k_c, k_l], axis=2)
            v_all = jnp.concatenate([v_c, v_l], axis=2)
            nb = S // Q_BLOCK
            qb = jnp.moveaxis(q_l.reshape(B_, ATT_KV_HEADS, ATT_GROUP, nb, Q_BLOCK, HEAD_DIM), 3, 0)
            ob = lax.map(lambda blk: grouped_attend(blk, k_all, v_all), qb)
            o_lat = ob.transpose(1, 0, 4, 2, 3, 5).reshape(B_, S, Q_WIDTH) @ w_att_out[j]
            if not last:
                Lc = a_ctx.shape[1]
                q_c = attn_q(a_ctx, w_q, g_q[j]).reshape(B_, ATT_KV_HEADS, ATT_GROUP, Lc, HEAD_DIM)
                o_c = grouped_attend(q_c, k_c, v_c)
                o_ctx = o_c.transpose(0, 3, 1, 2, 4).reshape(B_, Lc, Q_WIDTH) @ w_att_out[j]
        h_lat = h_lat + ga1 * o_lat
        f_lat = rms_norm(h_lat, g_norm_ffn[i]) * (1 + sc2) + sh2
        h_lat = h_lat + ga2 * conv_ffn(f_lat, w_ffn_up[i], w_ffn_conv[i], b_ffn_conv[i], w_ffn_down[i])
        if not last:
            h_ctx = h_ctx + cga1 * o_ctx
            f_ctx = rms_norm(h_ctx, g_norm_ffn[i]) * (1 + csc2) + csh2
            h_ctx = h_ctx + cga2 * conv_ffn(f_ctx, w_ffn_up[i], w_ffn_conv[i], b_ffn_conv[i], w_ffn_down[i])
    return rms_norm(h_lat, g_norm_final)
```

```python
import numpy as np
import ml_dtypes
import concourse.bass as bass
import concourse.mybir as mybir
from concourse.bass_utils import run_bass_kernel_spmd
from contextlib import ExitStack, contextmanager

F32 = mybir.dt.float32
BF16 = mybir.dt.bfloat16
AF = mybir.ActivationFunctionType
ALU = mybir.AluOpType
NPBF = ml_dtypes.bfloat16

D = 1024
KC = 8
SEQ = 8192
CTX = 256
DFF = 2816
EPS = 1e-6


class Tok:
    __slots__ = ("w", "r")

    def __init__(self):
        self.w = None
        self.r = {}


LAST_PROG = [None]


class Prog:
    ENGS = ("tensor", "vector", "scalar", "gpsimd", "sync")
    SELF_SYNC = True
    NDMA = 24

    def __init__(self, nc):
        self.nc = nc
        self.streams = {e: [] for e in self.ENGS}
        self.count = {e: 0 for e in self.ENGS}
        self.sem = {e: nc.alloc_semaphore("prog_" + e) for e in self.ENGS}
        self.waited = {e: {} for e in self.ENGS}
        self.dsem = [nc.alloc_semaphore(f"dma_{i}") for i in range(self.NDMA)]
        self.dcount = [0] * self.NDMA
        self.dnext = 0
        self.swsems = []

    def _need(self, eng, deps, skip_self):
        for key, (sem, val) in deps.items():
            if skip_self and sem is self.sem[eng]:
                continue
            if self.waited[eng].get(key, 0) < val:
                self.waited[eng][key] = val
                self.streams[eng].append(("wait", sem, val))

    @staticmethod
    def _collect(reads, writes):
        deps = {}

        def add(c):
            if c is None:
                return
            k = id(c[0])
            if k not in deps or deps[k][1] < c[1]:
                deps[k] = c
        for t in reads:
            add(t.w)
        for t in writes:
            add(t.w)
            for c in t.r.values():
                add(c)
        return deps

    @staticmethod
    def _mark(cid, reads, writes):
        k = id(cid[0])
        for t in reads:
            if k not in t.r or t.r[k][1] < cid[1]:
                t.r[k] = cid
        for t in writes:
            t.w = cid
            t.r = {}

    def op(self, eng, fn, reads=(), writes=(), skip_self=None):
        if skip_self is None:
            skip_self = (eng == "tensor") or (not self.SELF_SYNC)
        deps = self._collect(reads, writes)
        self._need(eng, deps, skip_self)
        self.count[eng] += 1
        cid = (self.sem[eng], self.count[eng])
        self.streams[eng].append(("op", fn, self.sem[eng], 1))
        self._mark(cid, reads, writes)
        return cid

    def dma(self, fn, reads=(), writes=(), queue="sync"):
        if queue == "gpsimd":
            deps = self._collect(reads, writes)
            self._need(queue, deps, False)
            sem = self.nc.alloc_semaphore(f"swdma_{len(self.swsems)}")
            self.swsems.append(sem)
            cid = (sem, 16)
            self.streams[queue].append(("op", fn, sem, 16))
            self._mark(cid, reads, writes)
            return cid
        deps = self._collect(reads, writes)
        i = self.dnext
        self.dnext = (self.dnext + 1) % self.NDMA
        if self.dcount[i] > 0:
            deps[id(self.dsem[i])] = (self.dsem[i], self.dcount[i])
        self._need(queue, deps, False)
        self.dcount[i] += 16
        cid = (self.dsem[i], self.dcount[i])
        self.streams[queue].append(("op", fn, self.dsem[i], 16))
        self._mark(cid, reads, writes)
        return cid

    def barrier(self):
        for e in self.ENGS:
            deps = {}
            for o in self.ENGS:
                if o != e and self.count[o] > 0:
                    deps[id(self.sem[o])] = (self.sem[o], self.count[o])
            for i in range(self.NDMA):
                if self.dcount[i] > 0:
                    deps[id(self.dsem[i])] = (self.dsem[i], self.dcount[i])
            for sw in self.swsems:
                deps[id(sw)] = (sw, 16)
            self._need(e, deps, True)

    def finish(self):
        LAST_PROG[0] = self
        for i in range(self.NDMA):
            if self.dcount[i] > 0:
                self.streams["sync"].append(("wait", self.dsem[i], self.dcount[i]))
        for sw in self.swsems:
            self.streams["sync"].append(("wait", sw, 16))
        nc = self.nc
        streams = self.streams

        def run(engine, lst):
            for it in lst:
                if it[0] == "wait":
                    engine.wait_ge(it[1], it[2])
                else:
                    it[1](engine).then_inc(it[2], it[3])

        with nc.Block() as block:
            @block.tensor
            def _(e):
                run(e, streams["tensor"])

            @block.vector
            def _(e):
                run(e, streams["vector"])

            @block.scalar
            def _(e):
                run(e, streams["scalar"])

            @block.gpsimd
            def _(e):
                run(e, streams["gpsimd"])

            @block.sync
            def _(e):
                run(e, streams["sync"])


class Buf:
    __slots__ = ("t", "k")

    def __init__(self, t):
        self.t = t
        self.k = Tok()

    def __getitem__(self, idx):
        return self.t[idx]


class Rot:
    def __init__(self, bufs):
        self.bufs = bufs
        self.i = 0

    def next(self):
        b = self.bufs[self.i]
        self.i = (self.i + 1) % len(self.bufs)
        return b


class B:
    def __init__(self, nacc=2):
        self.nc = bass.Bass("TRN2", target_bir_lowering=False)
        self.P = Prog(self.nc)
        self.n = 0
        self.stack = None
        self.allb = [Buf(self.nc.alloc_psum_tensor(f"bank{i}", [128, 512], F32)) for i in range(8)]
        self.set_acc(nacc)

    def set_acc(self, nacc):
        self.banks = Rot(self.allb[:8 - nacc])
        self.accb = Rot(self.allb[8 - nacc:])

    def din(self, name, shape, dt=F32):
        return self.nc.dram_tensor(name, list(shape), dt, kind="ExternalInput").ap()

    def dout(self, name, shape, dt=F32):
        return self.nc.dram_tensor(name, list(shape), dt, kind="ExternalOutput").ap()

    def sb(self, shape, dt, name=None):
        self.n += 1
        nm = f"sb{self.n}_{name or ''}"
        if self.stack is not None:
            return Buf(self.stack.enter_context(self.nc.sbuf_tensor(nm, list(shape), dt)))
        return Buf(self.nc.alloc_sbuf_tensor(nm, list(shape), dt))

    def push_scope(self):
        self._saved = getattr(self, "_saved", [])
        self._saved.append(self.stack)
        self.stack = ExitStack()

    def pop_scope(self):
        self.P.barrier()
        self.stack.close()
        self.stack = self._saved.pop()

    @contextmanager
    def scope(self):
        prev = self.stack
        self.stack = ExitStack()
        try:
            yield
        finally:
            self.P.barrier()
            self.stack.close()
            self.stack = prev

    def rot(self, n, shape, dt):
        return Rot([self.sb(shape, dt) for _ in range(n)])

    def load(self, dst, dst_ap, src_ap, queue="sync", slow=False):
        if slow:
            self.P.dma(lambda e: e.dma_start(out=dst_ap, in_=src_ap, allow_slow_non_contiguous=True), writes=[dst.k], queue=queue)
        else:
            self.P.dma(lambda e: e.dma_start(out=dst_ap, in_=src_ap), writes=[dst.k], queue=queue)

    def store(self, dst_ap, src, src_ap, queue="sync"):
        self.P.dma(lambda e: e.dma_start(out=dst_ap, in_=src_ap), reads=[src.k], queue=queue)

    def mm(self, bank, out_ap, lhsT, rhs, start, stop, reads):
        self.P.op("tensor", lambda e: e.matmul(out_ap, lhsT=lhsT, rhs=rhs, start=start, stop=stop),
                  reads=reads, writes=[bank.k])

    def act(self, out_ap, in_ap, func, reads, writes, scale=1.0, bias=0.0, eng="scalar"):
        self.P.op(eng, lambda e: e.activation(out=out_ap, in_=in_ap, func=func, scale=scale, bias=bias),
                  reads=reads, writes=writes)

    def tt(self, out_ap, in0, in1, op, reads, writes, eng="vector"):
        self.P.op(eng, lambda e: e.tensor_tensor(out=out_ap, in0=in0, in1=in1, op=op), reads=reads, writes=writes)

    def stt(self, out_ap, in0, scalar, in1, op0, op1, reads, writes):
        self.P.op("vector", lambda e: e.scalar_tensor_tensor(out=out_ap, in0=in0, scalar=scalar, in1=in1, op0=op0, op1=op1),
                  reads=reads, writes=writes)

    def ts(self, out_ap, in0, s1, s2, op0, op1, reads, writes, eng="vector"):
        if op1 is None:
            self.P.op(eng, lambda e: e.tensor_scalar(out=out_ap, in0=in0, scalar1=s1, scalar2=None, op0=op0),
                      reads=reads, writes=writes)
        else:
            self.P.op(eng, lambda e: e.tensor_scalar(out=out_ap, in0=in0, scalar1=s1, scalar2=s2, op0=op0, op1=op1),
                      reads=reads, writes=writes)

    def copy(self, out_ap, in_ap, reads, writes, eng="vector"):
        self.P.op(eng, lambda e: e.tensor_copy(out=out_ap, in_=in_ap), reads=reads, writes=writes)

    def recip(self, out_ap, in_ap, reads, writes):
        self.P.op("vector", lambda e: e.reciprocal(out=out_ap, in_=in_ap), reads=reads, writes=writes)

    def memset(self, buf, ap, val, eng="gpsimd"):
        self.P.op(eng, lambda e: e.memset(ap, val), writes=[buf.k])

    def consts(self, ones_ap):
        self.ones = self.sb([128, 128], BF16, "ones_bf")
        self.load(self.ones, self.ones[:], ones_ap)
        self.epsb = self.sb([128, 1], F32, "epsb")
        self.memset(self.epsb, self.epsb[:], EPS)

    def adaln(self, cT_ap, wmod_ap, bmodT_ap, gains):
        cT = self.sb([128, KC, 2], F32)
        self.load(cT, cT[:], cT_ap)
        sc = self.sb([128, KC, 2], BF16)
        self.act(sc[:], cT[:], AF.Silu, [cT.k], [sc.k])
        bm = self.sb([128, 48], F32)
        self.load(bm, bm[:], bmodT_ap)
        mod = self.sb([128, 48, 2], F32)
        with self.scope():
            wrot = self.rot(3, [128, KC, 512], BF16)
            wv = wmod_ap.rearrange("(kc p) f -> p kc f", p=128)
            bank = self.banks.next()
            for cb in range(12):
                w = wrot.next()
                self.load(w, w[:], wv[:, :, cb * 512:(cb + 1) * 512], queue="gpsimd")
                for j in range(4):
                    fc = cb * 4 + j
                    for kc in range(KC):
                        self.mm(bank, bank[:, fc * 2:fc * 2 + 2], w[:, kc, j * 128:(j + 1) * 128], sc[:, kc, :],
                                kc == 0, kc == KC - 1, [w.k, sc.k])
            bv = bank[:, 0:96].rearrange("p (f c) -> p f c", c=2)
            for c in range(2):
                self.tt(mod[:, :, c], bv[:, :, c], bm[:], ALU.add, [bm.k], [mod.k, bank.k])
        outs = []
        for (g_ap, kind) in gains:
            g = self.sb([128, KC], F32)
            self.load(g, g[:], g_ap)
            gsc = self.sb([128, KC, 2], F32)
            for c in range(2):
                self.stt(gsc[:, :, c], mod[:, kind * 8:(kind + 1) * 8, c], 1.0, g[:], ALU.add, ALU.mult,
                         [mod.k, g.k], [gsc.k])
            outs.append(gsc)
        return mod, outs

    def norm_mod(self, x, T, gsc, mod, shkind, col, a_out, sq, rstd, tmp):
        self.norm_a(x, T, sq)
        self.norm_b(x, T, gsc, mod, shkind, col, a_out, sq, rstd, tmp)

    def norm_a(self, x, T, sq):
        self.act(sq[:, :, 0:T], x[:, :, 0:T], AF.Square, [x.k], [sq.k])

    def norm_b(self, x, T, gsc, mod, shkind, col, a_out, sq, rstd, tmp):
        bank = self.banks.next()
        for kc in range(KC):
            self.mm(bank, bank[:, 0:T], self.ones[:], sq[:, kc, 0:T], kc == 0, kc == KC - 1, [sq.k, self.ones.k])
        self.act(rstd[:, 0:T], bank[:, 0:T], AF.Ln, [self.epsb.k], [rstd.k, bank.k], scale=1.0 / D, bias=self.epsb[:])
        self.act(rstd[:, 0:T], rstd[:, 0:T], AF.Exp, [], [rstd.k], scale=-0.5)
        for kc in range(KC):
            wr = [tmp.k] if tmp is not x else [x.k]
            self.tt(tmp[:, kc, 0:T], x[:, kc, 0:T], rstd[:, 0:T], ALU.mult, [x.k, rstd.k], wr)
        for kc in range(KC):
            if mod is None:
                self.ts(a_out[:, kc, 0:T], tmp[:, kc, 0:T], gsc[:, kc, col:col + 1], None, ALU.mult, None,
                        [tmp.k, gsc.k], [a_out.k], eng="gpsimd")
            else:
                self.ts(a_out[:, kc, 0:T], tmp[:, kc, 0:T], gsc[:, kc, col:col + 1], mod[:, shkind * 8 + kc, col:col + 1],
                        ALU.mult, ALU.add, [tmp.k, gsc.k, mod.k], [a_out.k], eng="gpsimd")


NTOK = CTX + SEQ
A_STOP = None
NCH = NTOK // 64


def consts_A():
    tau = np.arange(128)
    same = (tau[:, None] // 64) == (tau[None, :] // 64)
    s = -1.0 / 16.0
    tri = np.stack([
        (same & (tau[:, None] <= tau[None, :])) * s,
        (same & (tau[:, None] >= tau[None, :])) * s,
        (same & (tau[:, None] > tau[None, :])) * s,
        (same & (tau[:, None] < tau[None, :])) * s], 1).astype(np.float32)
    mF = (same & (tau[:, None] <= tau[None, :])).astype(np.float32)
    mB = (same & (tau[:, None] >= tau[None, :])).astype(np.float32)
    mask4 = np.stack([mF, mF, mB, mB], 1).astype(NPBF)
    ch = np.arange(128)
    ang = 2 * np.pi * np.outer(ch, ch) / 128.0
    csg = (np.concatenate([np.cos(ang), -np.sin(ang)], 1) / np.sqrt(128.0)).astype(NPBF)
    t1 = np.arange(128)[:, None, None]
    t2 = np.arange(64)[None, :, None]
    k1 = np.arange(128)[None, None, :]
    phi = 2 * np.pi * ((k1 * (64 * t1 + t2)) % 8192) / 8192.0
    tt = (np.concatenate([np.sin(phi), np.cos(phi), -np.sin(phi)], 2) / np.sqrt(128.0)).astype(NPBF)
    ident = np.eye(128, dtype=np.float32).astype(NPBF)
    t2 = np.arange(64)[:, None]
    k2 = np.arange(64)[None, :]
    th = 2 * np.pi * (t2 * k2 % 64) / 64.0
    L = np.zeros((64, 2, 64), np.float64)
    L[:, 0, :] = np.cos(th)
    L[:, 1, :] = np.sin(th)
    L = (L.reshape(128, 64) / 8.0).astype(NPBF)
    t = np.arange(256)[:, None]
    tp = np.arange(256)[None, :]
    psi = 2 * np.pi * (t * tp % 256) / 256.0
    c256 = (np.cos(psi) / 16.0).reshape(2, 128, 256).transpose(1, 0, 2).astype(NPBF)
    s256 = (np.sin(psi) / 16.0).reshape(2, 128, 256).transpose(1, 0, 2).astype(NPBF)
    cs256 = np.ascontiguousarray(np.stack([c256, s256], 1))
    return dict(ones=np.ones((128, 128), NPBF), tri=tri, mask4=mask4, csg=csg, tt=tt, ident=ident, Lm=L, cs256=cs256)


def stage_A(b, xT, ctxT, winfm, wintm, wg, bgp, gglaT, cst, mixT, row_f, row_g, mod, gsc):
    nc = b.nc
    c_tri, c_mask4, c_csg, c_tt, c_ident, c_L, c_cs256 = cst

    b.push_scope()
    Ust = b.sb([128, 2, SEQ], BF16, "Ust")
    Uc = b.sb([128, 2, CTX], BF16, "Uc")
    b.push_scope()
    Wfm = b.sb([128, KC, 896], BF16)
    Wtm = b.sb([128, KC, 384], BF16)
    b.load(Wfm, Wfm[:], winfm.rearrange("(kc p) f -> p kc f", p=128), queue="gpsimd")
    b.load(Wtm, Wtm[:], wintm.rearrange("(kc p) f -> p kc f", p=128), queue="gpsimd")
    Wg = b.sb([128, 2, 128], BF16)
    Bg = b.sb([128, 2, 128], BF16)
    b.load(Wg, Wg[:], wg, queue="gpsimd")
    b.load(Bg, Bg[:], bgp, queue="gpsimd")
    ggla = b.sb([128, 2], F32)
    b.load(ggla, ggla[:], gglaT)
    tri = b.sb([128, 4, 128], F32)
    b.load(tri, tri[:], c_tri)
    mask4 = b.sb([128, 4, 128], BF16)
    b.load(mask4, mask4[:], c_mask4)

    Sbst = b.sb([128, NCH, 128], BF16, "Sbst")
    S32 = b.sb([128, 128], F32, "S32")

    xr = b.rot(2, [128, KC, 512], F32)
    ar = b.rot(1, [128, KC, 512], BF16)
    rstdr = b.rot(1, [128, 512], F32)
    qTr = b.rot(1, [128, 512], F32)
    kTr = b.rot(1, [128, 512], F32)
    rsr = b.rot(2, [128, 2, 512], BF16)
    zTr = b.rot(2, [128, 512], BF16)
    KTMr = b.rot(2, [128, 4, 128], F32)
    VSr = b.rot(2, [128, 4, 256], BF16)
    e1r = b.rot(1, [128, 512], F32)
    nlr = b.rot(1, [128, 512], F32)
    Ebr = [b.rot(2, [128, 512], F32) for _ in range(2)]
    Enbr = b.rot(1, [128, 512], F32)
    Ecr = b.rot(1, [128, 512], F32)
    qpr = [[b.rot(2, [128, 512], BF16) for h in range(2)] for d in range(2)]
    ktr = [b.rot(2, [128, 512], BF16) for d in range(2)]
    khpr = [[b.rot(2, [128, 4, 128], BF16) for c in range(2)] for d in range(2)]
    for d in range(2):
        for h in range(2):
            for bf in qpr[d][h].bufs:
                b.memset(bf, bf[:], 0.0)
        for c in range(2):
            for bf in khpr[d][c].bufs:
                b.memset(bf, bf[:], 0.0)
    atmr = b.rot(2, [128, 4, 128], BF16)
    Sfr = b.rot(4, [128, 128], BF16)
    OCr = b.rot(1, [128, 4, 256], F32)
    OSQr = b.rot(1, [128, 4, 256], BF16)
    ORSr = b.rot(1, [128, 4, 256], F32)
    otr = b.rot(1, [128, 4, 128], F32)
    mgr = b.rot(2, [128, 2, 512], BF16)

    xv = xT.rearrange("(kc p) t -> p kc t", p=128)
    cv = ctxT.rearrange("(kc p) t -> p kc t", p=128)

    units = [(True, 0, CTX, 0, 0)] + [(False, i * 512, 512, CTX + i * 512, 4 + i * 8) for i in range(SEQ // 512)]

    def gates(zT, T, KTM, kT, qT, dirs, full):
        NS = T // 128
        res = {}
        for d in dirs:
            bank = b.banks.next()
            for s in range(NS):
                sl = slice(s * 128, (s + 1) * 128)
                b.mm(bank, bank[:, sl], zT[:, sl], Wg[:, d, :], True, False, [zT.k, Wg.k])
                b.mm(bank, bank[:, sl], b.ones[:], Bg[:, d, :], False, True, [b.ones.k, Bg.k])
            e1 = e1r.next()
            b.act(e1[:, 0:T], bank[:, 0:T], AF.Exp, [], [e1.k, bank.k], scale=-1.0)
            nl = nlr.next()
            b.act(nl[:, 0:T], e1[:, 0:T], AF.Ln, [e1.k], [nl.k], bias=1.0)
            bankB = b.banks.next()
            bankC = b.banks.next()
            for s in range(NS):
                sl = slice(s * 128, (s + 1) * 128)
                b.mm(bankB, bankB[:, sl], nl[:, sl], tri[:, d, :], True, True, [nl.k, tri.k])
            for s in range(NS):
                sl = slice(s * 128, (s + 1) * 128)
                b.mm(bankC, bankC[:, sl], tri[:, 2 + d, :], nl[:, sl], True, True, [nl.k, tri.k])
            Eb = Ebr[d].next()
            b.act(Eb[:, 0:T], bankB[:, 0:T], AF.Exp, [], [Eb.k, bankB.k])
            Ec = Ecr.next()
            b.act(Ec[:, 0:T], bankC[:, 0:T], AF.Exp, [], [Ec.k, bankC.k])
            r = dict(Eb=Eb)
            if full:
                Enb = Enbr.next()
                b.act(Enb[:, 0:T], bankB[:, 0:T], AF.Exp, [], [Enb.k, bankB.k], scale=-1.0)
                qp = []
                for h in range(2):
                    q = qpr[d][h].next()
                    hs = slice(h * 64, (h + 1) * 64)
                    b.stt(q[hs, 0:T], qT[hs, 0:T], 0.125, Eb[hs, 0:T], ALU.mult, ALU.mult, [qT.k, Eb.k], [q.k])
                    qp.append(q)
                kt = ktr[d].next()
                b.tt(kt[:, 0:T], kT[:, 0:T], Enb[:, 0:T], ALU.mult, [kT.k, Enb.k], [kt.k])
                r.update(qp=qp, kt=kt)
            khp = []
            for c in range(2):
                kh = khpr[d][c].next()
                cs = slice(c * 64, (c + 1) * 64)
                b.tt(kh[cs, 0:NS, :].rearrange("p s f -> p (s f)"), KTM[cs, 0:NS, :].rearrange("p s f -> p (s f)"),
                     Ec[cs, 0:T], ALU.mult, [KTM.k, Ec.k], [kh.k])
                khp.append(kh)
            r.update(khp=khp)
            res[d] = r
        return res

    def u_mm(bank, slot, khp_c, s, VS):
        b.mm(bank, bank[:, slot * 256:(slot + 1) * 256], khp_c[:, s, :], VS[:, s, :], True, True, [khp_c.k, VS.k])

    def s_update(bank, slot, Dap, Dk, shadow, shadow_ap):
        for h in range(2):
            hs = slice(h * 64, (h + 1) * 64)
            b.stt(S32[hs, :], S32[hs, :], Dap[hs, :], bank[hs, slot * 256 + h * 128:slot * 256 + (h + 1) * 128],
                  ALU.mult, ALU.add, [Dk], [S32.k, bank.k])
        b.copy(shadow_ap, S32[:], [S32.k], [shadow.k])

    def xload(u):
        is_ctx, t0, T, col0, g0 = u
        x = xr.next()
        b.load(x, x[:, :, 0:T], (cv if is_ctx else xv)[:, :, t0:t0 + T])
        return x

    def front_norm(u, x):
        is_ctx, t0, T, col0, g0 = u
        a = ar.next()
        rstd = rstdr.next()
        b.norm_mod(x, T, gsc, mod, 0, 1 if is_ctx else 0, a, a, rstd, x)
        return dict(a=a)

    def front_proj(u, full, out):
        is_ctx, t0, T, col0, g0 = u
        a = out["a"]
        if full:
            qT = qTr.next()
            kT = kTr.next()
            rs = rsr.next()
            for m in range(6):
                bank = b.banks.next()
                for kc in range(KC):
                    b.mm(bank, bank[:, 0:T], Wfm[:, kc, m * 128:(m + 1) * 128], a[:, kc, 0:T], kc == 0, kc == KC - 1,
                         [Wfm.k, a.k])
                if m == 0:
                    b.act(qT[:, 0:T], bank[:, 0:T], AF.Copy, [], [qT.k, bank.k])
                elif m == 1:
                    b.act(kT[:, 0:T], bank[:, 0:T], AF.Copy, [], [kT.k, bank.k])
                elif m < 4:
                    b.act(rs[:, m - 2, 0:T], bank[:, 0:T], AF.Silu, [], [rs.k, bank.k])
                else:
                    dst = Uc if is_ctx else Ust
                    b.copy(dst[:, m - 4, t0:t0 + T], bank[:, 0:T], [], [dst.k, bank.k])
            out.update(qT=qT, kT=kT, rs=rs)
        zT = zTr.next()
        bank = b.banks.next()
        for kc in range(KC):
            b.mm(bank, bank[:, 0:T], Wfm[:, kc, 768:896], a[:, kc, 0:T], kc == 0, kc == KC - 1, [Wfm.k, a.k])
        b.copy(zT[:, 0:T], bank[:, 0:T], [], [zT.k, bank.k])
        out.update(zT=zT)
        KTM = KTMr.next()
        VS = VSr.next()
        for s in range(T // 128):
            bank = b.banks.next()
            for kc in range(KC):
                b.mm(bank, bank[:, 0:384], a[:, kc, s * 128:(s + 1) * 128], Wtm[:, kc, :], kc == 0, kc == KC - 1, [a.k, Wtm.k])
            b.act(KTM[:, s, :], bank[:, 0:128], AF.Copy, [], [KTM.k, bank.k])
            b.copy(VS[:, s, :], bank[:, 128:384], [], [VS.k, bank.k])
        out.update(KTM=KTM, VS=VS)
        return out

    order = [3, 2, 1, 0] + list(range(NCH - 1, 3, -1))
    pass1_next = {order[i]: order[i + 1] for i in range(len(order) - 1)}
    roleA = Rot(b.allb[0:2])
    roleU = Rot(b.allb[2:4])
    roleO = Rot(b.allb[4:5])
    b.banks = Rot(b.allb[5:8])

    b.memset(S32, S32[:], 0.0)
    first = True
    ulist = ([] if A_STOP == 'adaln' else [units[0]] + units[:0:-1])
    xn = xload(ulist[0]) if ulist else None
    for ui, u in enumerate(ulist):
        is_ctx, t0, T, col0, g0 = u
        xc = xn
        xn = xload(ulist[ui + 1]) if ui + 1 < len(ulist) else None
        f = fn_ if ui > 0 else front_norm(u, xc)
        front_proj(u, False, f)
        g = gates(f["zT"], T, f["KTM"], None, None, [1], False)[1]
        fn_ = front_norm(ulist[ui + 1], xn) if xn is not None else None
        for s in range(T // 128 - 1, -1, -1):
            bankU = roleU.next()
            for c in (1, 0):
                u_mm(bankU, c, g["khp"][c], s, f["VS"])
            for c in (1, 0):
                gci = g0 + s * 2 + c
                if first:
                    b.copy(Sbst[:, gci, :], S32[:], [S32.k], [Sbst.k])
                    first = False
                col = s * 128 + c * 64
                nxt = pass1_next.get(gci)
                s_update(bankU, c, g["Eb"][:, col:col + 1], g["Eb"].k, Sbst if nxt is not None else Sfr.bufs[0],
                         Sbst[:, nxt, :] if nxt is not None else Sfr.bufs[0][:])

    b.memset(S32, S32[:], 0.0)
    Sf = Sfr.next()
    b.memset(Sf, Sf[:], 0.0)
    ulist = ([] if A_STOP in ('adaln', 'pass1') else units)
    xn = xload(ulist[0]) if ulist else None
    for ui, u in enumerate(ulist):
        is_ctx, t0, T, col0, g0 = u
        NS = T // 128
        xc = xn
        xn = xload(ulist[ui + 1]) if ui + 1 < len(ulist) else None
        f = fn_ if ui > 0 else front_norm(u, xc)
        front_proj(u, True, f)
        VS = f["VS"]
        g = gates(f["zT"], T, f["KTM"], f["kT"], f["qT"], [0, 1], True)
        fn_ = front_norm(ulist[ui + 1], xn) if xn is not None else None
        mg = mgr.next()
        OC = OCr.next()

        def stage_x(s):
            sl = slice(s * 128, (s + 1) * 128)
            bankA = roleA.next()
            for d in range(2):
                for h in range(2):
                    i = d * 2 + h
                    b.mm(bankA, bankA[:, i * 128:(i + 1) * 128], g[d]["kt"][:, sl], g[d]["qp"][h][:, sl], True, True,
                         [g[d]["kt"].k, g[d]["qp"][h].k])
            bankU = roleU.next()
            for c in range(2):
                u_mm(bankU, c, g[0]["khp"][c], s, VS)
            return bankA, bankU

        nxt_x = stage_x(0)
        for s in range(NS):
            bankA, bankU = nxt_x
            if s + 1 < NS:
                nxt_x = stage_x(s + 1)
            atm = atmr.next()
            b.tt(atm[:].rearrange("p a t -> p (a t)"), bankA[:, 0:512], mask4[:].rearrange("p a t -> p (a t)"), ALU.mult,
                 [mask4.k], [atm.k, bankA.k])
            Sin = [Sf]
            for c in range(2):
                col = s * 128 + c * 64 + 63
                Sn = Sfr.next()
                s_update(bankU, c, g[0]["Eb"][:, col:col + 1], g[0]["Eb"].k, Sn, Sn[:])
                Sin.append(Sn)
            Sf = Sin[2]
            bankO = roleO.next()
            for h in range(2):
                o = bankO[:, h * 128:(h + 1) * 128]
                vh = VS[:, s, h * 128:(h + 1) * 128]
                b.mm(bankO, o, vh, atm[:, 0 * 2 + h, :], True, False, [VS.k, atm.k])
                b.mm(bankO, o, vh, atm[:, 1 * 2 + h, :], False, False, [VS.k, atm.k])
                for c in range(2):
                    gci = g0 + s * 2 + c
                    cs = slice(s * 128 + c * 64, s * 128 + (c + 1) * 64)
                    b.mm(bankO, bankO[:, h * 128 + c * 64:h * 128 + (c + 1) * 64], Sbst[:, gci, :],
                         g[1]["qp"][h][:, cs], False, False, [Sbst.k, g[1]["qp"][h].k])
                for c in range(2):
                    cs = slice(s * 128 + c * 64, s * 128 + (c + 1) * 64)
                    b.mm(bankO, bankO[:, h * 128 + c * 64:h * 128 + (c + 1) * 64], Sin[c][:],
                         g[0]["qp"][h][:, cs], False, c == 1, [Sin[c].k, g[0]["qp"][h].k])
            b.act(OC[:, s, :], bankO[:, 0:256], AF.Copy, [], [OC.k, bankO.k])
        W = NS * 256
        OCf = OC[:, 0:NS, :].rearrange("p s f -> p (s f)")
        OSQ = OSQr.next()
        OSQf = OSQ[:, 0:NS, :].rearrange("p s f -> p (s f)")
        b.act(OSQf, OCf, AF.Square, [OC.k], [OSQ.k])
        ORS = ORSr.next()
        ORSf = ORS[:, 0:NS, :].rearrange("p s f -> p (s f)")
        for j in range(0, W, 512):
            w = min(512, W - j)
            bankS = b.banks.next()
            b.mm(bankS, bankS[:, 0:w], b.ones[:], OSQf[:, j:j + w], True, True, [b.ones.k, OSQ.k])
            b.act(ORSf[:, j:j + w], bankS[:, 0:w], AF.Ln, [b.epsb.k], [ORS.k, bankS.k], scale=1.0 / 128, bias=b.epsb[:])
        b.act(ORSf, ORSf, AF.Exp, [], [ORS.k], scale=-0.5)
        for h in range(2):
            ot = otr.next()
            b.stt(ot[:, 0:NS, :], OC[:, 0:NS, h * 128:(h + 1) * 128], ggla[:, h:h + 1], ORS[:, 0:NS, h * 128:(h + 1) * 128],
                  ALU.mult, ALU.mult, [OC.k, ggla.k, ORS.k], [ot.k])
            b.tt(mg[:, h, 0:T], ot[:, 0:NS, :].rearrange("p s f -> p (s f)"), f["rs"][:, h, 0:T], ALU.mult,
                 [ot.k, f["rs"].k], [mg.k], eng="gpsimd")
        for h in range(2):
            b.store(mixT[row_g + h * 128:row_g + (h + 1) * 128, col0:col0 + T], mg, mg[:, h, 0:T])

    b.pop_scope()
    b.set_acc(2)
    csg = b.sb([128, 256], BF16)
    b.load(csg, csg[:], c_csg)
    ident = b.sb([128, 128], BF16)
    b.load(ident, ident[:], c_ident)
    Lm = b.sb([128, 64], BF16)
    b.load(Lm, Lm[:], c_L)
    cs256 = b.sb([128, 2, 2, 256], BF16)
    b.load(cs256, cs256[:], c_cs256)
    TT = b.sb([128, 64, 384], BF16, "TT")
    b.load(TT, TT[:], c_tt)
    E0 = b.sb([128, 64, 256], BF16, "E0")
    D1 = b.sb([128, 64, 2, 128], BF16, "D1")
    G8r = b.rot(2, [128, 8, 128], BF16)
    Yr = b.rot(2, [128, 64, 128], BF16)
    Ycr = b.rot(2, [128, 256], BF16)
    evi = [0]

    def evac(out_ap, in_ap, wr):
        evi[0] += 1
        if evi[0] % 2 == 0:
            b.copy(out_ap, in_ap, [], wr)
        else:
            b.act(out_ap, in_ap, AF.Copy, [], wr)

    for g in ([] if A_STOP else range(2)):
        E0c = E0
        for tb in range(2):
            bank = b.banks.next()
            b.mm(bank, bank[:, 0:256], Uc[:, g, tb * 128:(tb + 1) * 128], csg[:], True, True, [Uc.k, csg.k])
            evac(E0c[:, tb, :], bank[:, 0:256], [E0c.k, bank.k])
        bank = b.banks.next()
        n = 0
        for tb in range(2):
            for ri in range(2):
                b.mm(bank, bank[:, 0:256], E0c[:, tb, ri * 128:(ri + 1) * 128], cs256[:, ri, tb, :], n == 0, n == 3,
                     [E0c.k, cs256.k])
                n += 1
        Yc = Ycr.next()
        evac(Yc[:], bank[:, 0:256], [Yc.k, bank.k])
        b.store(mixT[row_f + g * 128:row_f + (g + 1) * 128, 0:CTX], Yc, Yc[:])
        Uv = Ust[:, g, :].rearrange("p (a c) -> p a c", c=64)
        for t2 in range(0, 64, 2):
            bank = b.banks.next()
            for j in range(2):
                b.mm(bank, bank[:, j * 256:(j + 1) * 256], Uv[:, :, t2 + j], csg[:], True, True, [Ust.k, csg.k])
            evac(E0[:, t2:t2 + 2, :].rearrange("p a f -> p (a f)"), bank[:, 0:512], [E0.k, bank.k])
        for t2 in range(0, 64, 2):
            bank = b.banks.next()
            for j in range(2):
                o = bank[:, j * 256:(j + 1) * 256]
                b.mm(bank, o, E0[:, t2 + j, 0:128], TT[:, t2 + j, 128:384], True, False, [E0.k, TT.k])
                b.mm(bank, o, E0[:, t2 + j, 128:256], TT[:, t2 + j, 0:256], False, True, [E0.k, TT.k])
            evac(D1[:, t2:t2 + 2, :, :].rearrange("p a r k -> p (a r k)"), bank[:, 0:512], [D1.k, bank.k])
        Y = Yr.next()
        for kb in range(16):
            G8 = G8r.next()
            for half in range(2):
                bankG = b.banks.next()
                for j in range(4):
                    k1 = kb * 8 + half * 4 + j
                    b.mm(bankG, bankG[:, j * 128:(j + 1) * 128], D1[:, :, :, k1], ident[:], True, True, [D1.k, ident.k])
                evac(G8[:, half * 4:(half + 1) * 4, :].rearrange("p a f -> p (a f)"), bankG[:, 0:512], [G8.k, bankG.k])
            bankY = b.accb.next()
            for kk in range(8):
                b.mm(bankY, bankY[:, kk * 64:(kk + 1) * 64], G8[:, kk, :], Lm[:], True, True, [G8.k, Lm.k])
            evac(Y[:, :, kb * 8:(kb + 1) * 8].rearrange("p k2 kk -> p kk k2"),
                 bankY[:, 0:512].rearrange("p (kk k2) -> p kk k2", k2=64), [Y.k, bankY.k])
        b.store(mixT[row_f + g * 128:row_f + (g + 1) * 128, CTX:NTOK], Y, Y[:].rearrange("p a c -> p (a c)"))
    b.pop_scope()


def build_A():
    b = B()
    nc = b.nc
    xT = b.din("xT", [D, SEQ])
    ctxT = b.din("ctxT", [D, CTX])
    cT = b.din("cT", [128, KC, 2])
    wmod = b.din("wmod", [D, 6 * D])
    bmodT = b.din("bmodT", [128, 48])
    gmixT = b.din("gmixT", [128, KC])
    winfm = b.din("winfm", [D, 896])
    wintm = b.din("wintm", [D, 384])
    wg = b.din("wg", [128, 2, 128])
    bgp = b.din("bgp", [128, 2, 128])
    gglaT = b.din("gglaT", [128, 2])
    c_ones = b.din("c_ones", [128, 128], BF16)
    c_tri = b.din("c_tri", [128, 4, 128])
    c_mask4 = b.din("c_mask4", [128, 4, 128], BF16)
    c_csg = b.din("c_csg", [128, 256], BF16)
    c_tt = b.din("c_tt", [128, 64, 384], BF16)
    c_ident = b.din("c_ident", [128, 128], BF16)
    c_L = b.din("c_L", [128, 64], BF16)
    c_cs256 = b.din("c_cs256", [128, 2, 2, 256], BF16)
    mixT = b.dout("mixT", [512, NTOK], BF16)

    b.consts(c_ones)
    mod, (gsc,) = b.adaln(cT, wmod, bmodT, [(gmixT, 1)])
    stage_A(b, xT, ctxT, winfm, wintm, wg, bgp, gglaT, (c_tri, c_mask4, c_csg, c_tt, c_ident, c_L, c_cs256), mixT, 0, 256, mod, gsc)
    b.P.finish()
    return nc


def prep_A(inp, bi, hp, cA):
    f32 = np.float32
    w_in = inp["w_even_in"][0]
    qs, ks, vs, rs, zs = 512, 768, 1024, 1536, 2048
    cols_fm = np.concatenate([
        np.arange(qs + 128 * hp, qs + 128 * hp + 128), np.arange(ks + 128 * hp, ks + 128 * hp + 128),
        np.arange(rs + 256 * hp, rs + 256 * hp + 256), np.arange(256 * hp, 256 * hp + 256)])
    winfm = np.zeros((D, 896), f32)
    winfm[:, 0:768] = w_in[:, cols_fm]
    winfm[:, 768:784] = w_in[:, zs:zs + 16]
    cols_tm = np.concatenate([np.arange(ks + 128 * hp, ks + 128 * hp + 128), np.arange(vs + 256 * hp, vs + 256 * hp + 256)])
    wintm = np.ascontiguousarray(w_in[:, cols_tm])
    wg = np.zeros((128, 2, 128), f32)
    wg[0:16] = inp["w_gla_gate"][0][:, :, 128 * hp:128 * hp + 128].transpose(1, 0, 2)
    bgp = np.zeros((128, 2, 128), f32)
    bgp[0] = inp["b_gla_gate"][0][:, 128 * hp:128 * hp + 128]
    cT = np.stack([inp["c"][bi].reshape(KC, 128).T, inp["c_ctx"].reshape(KC, 128).T], 2).astype(f32)
    m = dict(
        xT=np.ascontiguousarray(inp["x"][bi].T), ctxT=np.ascontiguousarray(inp["ctx"][bi].T),
        cT=np.ascontiguousarray(cT), wmod=inp["w_mod"][0], bmodT=np.ascontiguousarray(inp["b_mod"][0].reshape(48, 128).T),
        gmixT=np.ascontiguousarray(inp["g_norm_mix"][0].reshape(KC, 128).T),
        winfm=winfm, wintm=wintm, wg=wg, bgp=bgp,
        gglaT=np.ascontiguousarray(inp["g_gla_out"][0][256 * hp:256 * hp + 256].reshape(2, 128).T),
        c_ones=cA["ones"], c_tri=cA["tri"], c_mask4=cA["mask4"], c_csg=cA["csg"], c_tt=cA["tt"], c_ident=cA["ident"],
        c_L=cA["Lm"], c_cs256=cA["cs256"])
    return m


NL = SEQ // 2
NLS = NL + 2
TB1 = 510


def load_ffn_weights_p1(b, wout_ap, wup_ap, wcT_ap):
    Wout = b.sb([128, KC, D], BF16, "Wout")
    b.load(Wout, Wout[:], wout_ap.rearrange("(kc p) f -> p kc f", p=128), queue="gpsimd")
    Wup = b.sb([128, KC, 2 * DFF], BF16, "Wup")
    wv = wup_ap.rearrange("(kc p) f -> p kc f", p=128)
    for i in range(4):
        b.load(Wup, Wup[:, :, i * 1408:(i + 1) * 1408], wv[:, :, i * 1408:(i + 1) * 1408], queue="gpsimd")
    wc = b.sb([128, 44, 4], F32, "wc")
    b.load(wc, wc[:], wcT_ap)
    return Wout, Wup, wc


def ffn_p1(b, segs, Wout, Wup, wc, mod, ga1kind, gscF, shkind, emask, hid_d, hmid_d):
    Mr = b.rot(2, [128, KC, 512], BF16)
    hmr = b.rot(2, [128, KC, 512], F32)
    sqr = b.rot(1, [128, KC, 512], BF16)
    rstdr = b.rot(2, [128, 512], F32)
    fr = b.rot(2, [128, KC, 512], BF16)
    c1r = b.rot(3, [128, 512], F32)
    c2r = b.rot(3, [128, 512], F32)
    hr = b.rot(4, [128, 512], BF16)
    for bf in hmr.bufs:
        b.memset(bf, bf[:], 0.0)
    for bf in Mr.bufs:
        b.memset(bf, bf[:], 0.0)
    plan = []
    for (Msrc, xsrc, N, col, has_halo, cbase) in segs:
        for i in range((N + TB1 - 1) // TB1):
            plan.append((Msrc.rearrange("(kc p) t -> p kc t", p=128), xsrc.rearrange("(kc p) t -> p kc t", p=128),
                         N, col, has_halo, cbase, i * TB1, min(TB1, N - i * TB1)))

    def p1_load(item):
        Mv, xv, N, col, has_halo, cbase, s, T = item
        W = T + 2
        M = Mr.next()
        hm = hmr.next()
        lo = max(s, 1)
        hi = min(s + W, N + 1)
        for (dst, srcv) in ((M, Mv), (hm, xv)):
            b.load(dst, dst[:, :, lo - s:hi - s], srcv[:, :, lo - 1:hi - 1])
            if has_halo and s == 0:
                b.load(dst, dst[:, :, 0:1], srcv[:, :, N:N + 1], slow=True)
            if has_halo and s + W == N + 2:
                b.load(dst, dst[:, :, W - 1:W], srcv[:, :, N + 1:N + 2], slow=True)
        return M, hm

    def prep_a(item, bufs):
        Mv, xv, N, col, has_halo, cbase, s, T = item
        W = T + 2
        M, hm = bufs
        for m in range(KC):
            bank = b.banks.next()
            for kc in range(KC):
                b.mm(bank, bank[:, 0:W], Wout[:, kc, m * 128:(m + 1) * 128], M[:, kc, 0:W], kc == 0, kc == KC - 1,
                     [Wout.k, M.k])
            b.stt(hm[:, m, 0:W], bank[:, 0:W], mod[:, ga1kind * 8 + m, col:col + 1], hm[:, m, 0:W], ALU.mult, ALU.add,
                  [mod.k], [hm.k, bank.k])
        b.store(hmid_d.rearrange("(kc p) t -> p kc t", p=128)[:, :, cbase + s:cbase + s + T], hm, hm[:, :, 1:T + 1])
        sq = sqr.next()
        b.norm_a(hm, W, sq)
        return dict(item=item, hm=hm, sq=sq)

    def prep_b(c):
        Mv, xv, N, col, has_halo, cbase, s, T = c["item"]
        W = T + 2
        f = fr.next()
        rstd = rstdr.next()
        b.norm_b(c["hm"], W, gscF, mod, shkind, col, f, c["sq"], rstd, c["sq"])
        if s == 0:
            if has_halo:
                b.ts(f[:, :, 0:1], f[:, :, 0:1], emask[:, 0:1], None, ALU.mult, None, [emask.k], [f.k], eng="gpsimd")
            else:
                b.memset(f, f[:, :, 0:1], 0.0)
        if s + W == N + 2:
            if has_halo:
                b.ts(f[:, :, W - 1:W], f[:, :, W - 1:W], emask[:, 1:2], None, ALU.mult, None, [emask.k], [f.k], eng="gpsimd")
            else:
                b.memset(f, f[:, :, W - 1:W], 0.0)
        c["f"] = f

    def chunk(c, j):
        Mv, xv, N, col, has_halo, cbase, s, T = c["item"]
        W = T + 2
        f = c["f"]
        cv = []
        for part in range(2):
            cj = part * 22 + j
            bank = b.banks.next()
            for kc in range(KC):
                b.mm(bank, bank[:, 0:W], Wup[:, kc, cj * 128:(cj + 1) * 128], f[:, kc, 0:W], kc == 0, kc == KC - 1,
                     [Wup.k, f.k])
            c1 = c1r.next()
            b.act(c1[:, 0:T], bank[:, 1:T + 1], AF.Identity, [wc.k], [c1.k, bank.k], scale=wc[:, cj, 1:2], bias=wc[:, cj, 3:4])
            c2 = c2r.next()
            b.stt(c2[:, 0:T], bank[:, 0:T], wc[:, cj, 0:1], c1[:, 0:T], ALU.mult, ALU.add, [wc.k, c1.k], [c2.k, bank.k])
            b.stt(c1[:, 0:T], bank[:, 2:T + 2], wc[:, cj, 2:3], c2[:, 0:T], ALU.mult, ALU.add, [wc.k, c2.k], [c1.k, bank.k])
            cv.append(c1)
        sg = c2r.next()
        b.act(sg[:, 0:T], cv[0][:, 0:T], AF.Silu, [cv[0].k], [sg.k])
        h = hr.next()
        b.tt(h[:, 0:T], sg[:, 0:T], cv[1][:, 0:T], ALU.mult, [sg.k, cv[1].k], [h.k], eng="gpsimd")
        b.store(hid_d[j, :, cbase + s:cbase + s + T], h, h[:, 0:T])

    NJ = DFF // 128
    JA, JB = NJ - 8, NJ - 4
    loads = {0: p1_load(plan[0])}
    if len(plan) > 1:
        loads[1] = p1_load(plan[1])
    cur = prep_a(plan[0], loads.pop(0))
    prep_b(cur)
    for pi in range(len(plan)):
        if pi + 2 < len(plan):
            loads[pi + 2] = p1_load(plan[pi + 2])
        nxt_c = None
        for j in range(NJ):
            if j == JA and pi + 1 < len(plan):
                nxt_c = prep_a(plan[pi + 1], loads.pop(pi + 1))
            if j == JB and nxt_c is not None:
                prep_b(nxt_c)
            chunk(cur, j)
        cur = nxt_c


def ffn_p2_load(b, hid_d, hmid_d, c0, T, hidr, hmr):
    hid = hidr.next()
    b.load(hid, hid[:, :, 0:T], hid_d.rearrange("j p t -> p j t")[:, :, c0:c0 + T])
    hm = hmr.next()
    b.load(hm, hm[:, :, 0:T], hmid_d.rearrange("(kc p) t -> p kc t", p=128)[:, :, c0:c0 + T])
    return hid, hm


def ffn_p2_block(b, Wdn, mod, ga2kind, col, hid_d, hmid_d, c0, T, hidr, hmr, pre=None):
    hid, hm = pre if pre is not None else ffn_p2_load(b, hid_d, hmid_d, c0, T, hidr, hmr)
    NJ = DFF // 128
    for m in range(KC):
        bank = b.banks.next()
        for j in range(NJ):
            b.mm(bank, bank[:, 0:T], Wdn[:, j, m * 128:(m + 1) * 128], hid[:, j, 0:T], j == 0, j == NJ - 1, [Wdn.k, hid.k])
        b.stt(hm[:, m, 0:T], bank[:, 0:T], mod[:, ga2kind * 8 + m, col:col + 1], hm[:, m, 0:T], ALU.mult, ALU.add,
              [mod.k], [hm.k, bank.k])
    return hm


NCB = CTX + NL


def build_B():
    b = B()
    nc = b.nc
    MctxT = b.din("MctxT", [D, CTX], BF16)
    MlatT = b.din("MlatT", [D, NLS], BF16)
    ctxT = b.din("ctxT", [D, CTX])
    xlatT = b.din("xlatT", [D, NLS])
    cT = b.din("cT", [128, KC, 2])
    wmod0 = b.din("wmod0", [D, 6 * D])
    bmod0T = b.din("bmod0T", [128, 48])
    wmod1 = b.din("wmod1", [D, 6 * D])
    bmod1T = b.din("bmod1T", [128, 48])
    gffn0T = b.din("gffn0T", [128, KC])
    gmix1T = b.din("gmix1T", [128, KC])
    wout = b.din("wout", [D, D])
    wup = b.din("wup", [D, 2 * DFF])
    wcT = b.din("wcT", [128, 44, 4])
    wdn = b.din("wdn", [DFF, D])
    emask_d = b.din("emask", [128, 2])
    wqkv = b.din("wqkv", [D, 1536])
    gqk_d = b.din("gqk", [128, 4])
    ropeT = b.din("ropeT", [128, 2, NCB])
    c_rperm = b.din("c_rperm", [128, 128], BF16)
    c_ones = b.din("c_ones", [128, 128], BF16)
    h1T = b.dout("h1T", [D, NL])
    qT = b.dout("qT", [128, 8, NL], BF16)
    kT = b.dout("kT", [128, 2, NCB], BF16)
    vtm = b.dout("vtm", [NCB, 256], BF16)
    hid_d = nc.dram_tensor("hid_d", [22, 128, NCB], BF16).ap()
    hmid_d = nc.dram_tensor("hmid_d", [D, NCB], F32).ap()

    b.consts(c_ones)
    mod0, (gscF0,) = b.adaln(cT, wmod0, bmod0T, [(gffn0T, 4)])
    mod1, (gsc1,) = b.adaln(cT, wmod1, bmod1T, [(gmix1T, 1)])
    emask = b.sb([128, 2], F32)
    b.load(emask, emask[:], emask_d)

    b.push_scope()
    Wout, Wup, wc = load_ffn_weights_p1(b, wout, wup, wcT)
    ffn_p1(b, [(MctxT, ctxT, CTX, 1, False, 0), (MlatT, xlatT, NL, 0, True, CTX)], Wout, Wup, wc, mod0, 2, gscF0, 3, emask,
           hid_d, hmid_d)
    b.pop_scope()

    b.push_scope()
    Wdn = b.sb([128, 22, D], BF16, "Wdn")
    b.load(Wdn, Wdn[:], wdn.rearrange("(j p) f -> p j f", p=128), queue="gpsimd")
    Wqkv = b.sb([128, KC, 1536], BF16, "Wqkv")
    b.load(Wqkv, Wqkv[:], wqkv.rearrange("(kc p) f -> p kc f", p=128), queue="gpsimd")
    gqk = b.sb([128, 4], F32)
    b.load(gqk, gqk[:], gqk_d)
    roper = b.rot(2, [128, 2, 512], F32)
    rperm = b.sb([128, 128], BF16)
    b.load(rperm, rperm[:], c_rperm)
    epsq = b.sb([128, 1], F32)
    b.memset(epsq, epsq[:], 128 * EPS)
    hidr = b.rot(1, [128, 22, 512], BF16)
    hmr = b.rot(2, [128, KC, 512], F32)
    sqr = b.rot(1, [128, KC, 512], BF16)
    rstdr = b.rot(2, [128, 512], F32)
    a1r = b.rot(2, [128, KC, 512], BF16)
    qbr = b.rot(2, [128, 512], BF16)
    sqbr = b.rot(2, [128, 512], BF16)
    rqr = b.rot(2, [128, 512], F32)
    t1r = b.rot(2, [128, 512], F32)
    t2r = b.rot(2, [128, 512], F32)
    qsr = b.rot(2, [128, 10, 512], BF16)
    vstr = b.rot(3, [128, 256], BF16)
    blocks = [(1, 0, CTX, 0)] + [(0, i * 512, 512, CTX) for i in range(NL // 512)]
    pre = ffn_p2_load(b, hid_d, hmid_d, blocks[0][3] + blocks[0][1], blocks[0][2], hidr, hmr)
    for bi_, (col, t0, T, cbase) in enumerate(blocks):
        is_ctx = col == 1
        c0 = cbase + t0
        cur = pre
        if bi_ + 1 < len(blocks) and len(hidr.bufs) > 1:
            nb_ = blocks[bi_ + 1]
            pre = ffn_p2_load(b, hid_d, hmid_d, nb_[3] + nb_[1], nb_[2], hidr, hmr)
        hm = ffn_p2_block(b, Wdn, mod0, 5, col, hid_d, hmid_d, c0, T, hidr, hmr, pre=cur)
        if bi_ + 1 < len(blocks) and len(hidr.bufs) == 1:
            nb_ = blocks[bi_ + 1]
            pre = ffn_p2_load(b, hid_d, hmid_d, nb_[3] + nb_[1], nb_[2], hidr, hmr)
        if not is_ctx:
            b.store(h1T.rearrange("(kc p) t -> p kc t", p=128)[:, :, t0:t0 + T], hm, hm[:, :, 0:T])
        a1 = a1r.next()
        sq = sqr.next()
        rstd = rstdr.next()
        b.norm_mod(hm, T, gsc1, mod1, 0, col, a1, sq, rstd, sq)
        qs = qsr.next()
        rope = roper.next()
        b.load(rope, rope[:, :, 0:T], ropeT[:, :, c0:c0 + T])
        hds = list(range(8, 10) if is_ctx else range(10))

        def hproj(hd_):
            bk = b.banks.next()
            for kc in range(KC):
                b.mm(bk, bk[:, 0:T], Wqkv[:, kc, hd_ * 128:(hd_ + 1) * 128], a1[:, kc, 0:T], kc == 0, kc == KC - 1,
                     [Wqkv.k, a1.k])
            return bk
        assert len(b.banks.bufs) >= 6
        bank_n = hproj(hds[0])
        for hi_, hd in enumerate(hds):
            isq = hd < 8
            bank = bank_n
            qb = qbr.next()
            b.act(qb[:, 0:T], bank[:, 0:T], AF.Copy, [], [qb.k, bank.k])
            sqb = sqbr.next()
            b.act(sqb[:, 0:T], bank[:, 0:T], AF.Square, [], [sqb.k, bank.k])
            if hi_ + 1 < len(hds):
                bank_n = hproj(hds[hi_ + 1])
            bankS = b.banks.next()
            b.mm(bankS, bankS[:, 0:T], b.ones[:], sqb[:, 0:T], True, True, [b.ones.k, sqb.k])
            bankP = b.banks.next()
            b.mm(bankP, bankP[:, 0:T], rperm[:], qb[:, 0:T], True, True, [rperm.k, qb.k])
            rq = rqr.next()
            if isq:
                b.act(rq[:, 0:T], bankS[:, 0:T], AF.Ln, [epsq.k], [rq.k, bankS.k], scale=1.0, bias=epsq[:])
            else:
                b.act(rq[:, 0:T], bankS[:, 0:T], AF.Ln, [b.epsb.k], [rq.k, bankS.k], scale=1.0 / 128, bias=b.epsb[:])
            b.act(rq[:, 0:T], rq[:, 0:T], AF.Exp, [], [rq.k], scale=-0.5)
            gi = 0 if isq else 2
            t1 = t1r.next()
            b.stt(t1[:, 0:T], qb[:, 0:T], gqk[:, gi:gi + 1], rope[:, 0, 0:T], ALU.mult, ALU.mult, [qb.k, gqk.k, rope.k], [t1.k])
            t2 = t2r.next()
            b.stt(t2[:, 0:T], bankP[:, 0:T], gqk[:, gi + 1:gi + 2], rope[:, 1, 0:T], ALU.mult, ALU.mult, [gqk.k, rope.k],
                  [t2.k, bankP.k])
            b.tt(t1[:, 0:T], t1[:, 0:T], t2[:, 0:T], ALU.add, [t2.k], [t1.k])
            b.tt(qs[:, hd, 0:T], t1[:, 0:T], rq[:, 0:T], ALU.mult, [t1.k, rq.k], [qs.k])
        if not is_ctx:
            b.store(qT[:, :, t0:t0 + T], qs, qs[:, 0:8, 0:T])
        b.store(kT[:, :, c0:c0 + T], qs, qs[:, 8:10, 0:T])
        for s in range(T // 128):
            bank = b.banks.next()
            for kc in range(KC):
                b.mm(bank, bank[:, 0:256], a1[:, kc, s * 128:(s + 1) * 128], Wqkv[:, kc, 1280:1536], kc == 0, kc == KC - 1,
                     [a1.k, Wqkv.k])
            vst = vstr.next()
            b.copy(vst[:], bank[:, 0:256], [], [vst.k, bank.k])
            b.store(vtm[c0 + s * 128:c0 + (s + 1) * 128, :], vst, vst[:])
    b.pop_scope()
    b.P.finish()
    return nc


def rope_tables():
    t = np.arange(SEQ)
    r = (t // 64).astype(np.float32)
    c = (t % 64).astype(np.float32)
    inv = (np.float32(10000.0) ** (-np.arange(0, 64, 2, dtype=np.float32) / np.float32(64))).astype(np.float32)
    ang = np.concatenate([r[:, None] * inv, c[:, None] * inv], -1)
    cos, sin = np.cos(ang), np.sin(ang)
    cosT = np.concatenate([cos, cos], 1).T
    sinT = np.concatenate([-sin, sin], 1).T
    return cosT.astype(np.float32), sinT.astype(np.float32)


def halo_cols(arrT, L0):
    out = np.zeros((arrT.shape[0], NLS), arrT.dtype)
    out[:, 0:NL] = arrT[:, L0:L0 + NL]
    if L0 > 0:
        out[:, NL] = arrT[:, L0 - 1]
    if L0 + NL < SEQ:
        out[:, NL + 1] = arrT[:, L0 + NL]
    return out


def mod_inputs(inp, bi):
    f32 = np.float32
    cT = np.stack([inp["c"][bi].reshape(KC, 128).T, inp["c_ctx"].reshape(KC, 128).T], 2).astype(f32)
    return np.ascontiguousarray(cT)


def vecT(v, n=KC):
    return np.ascontiguousarray(np.asarray(v).reshape(n, 128).T.astype(np.float32))


def conv_pack(inp, layer):
    wc = np.zeros((128, 44, 4), np.float32)
    wc[:, :, 0:3] = inp["w_ffn_conv"][layer].reshape(3, 44, 128).transpose(2, 1, 0)
    wc[:, :, 3] = inp["b_ffn_conv"][layer].reshape(44, 128).T
    return wc


def prep_B(inp, bi, th, mix_pair, cA, rt):
    L0 = th * NL
    Mfull = np.concatenate([mix_pair[0][0:256], mix_pair[1][0:256], mix_pair[0][256:512], mix_pair[1][256:512]], 0)
    cosT, sinT = rt
    rope = np.zeros((128, 2, NCB), np.float32)
    rope[:, 0, 0:CTX] = 1.0
    rope[:, 0, CTX:] = cosT[:, L0:L0 + NL]
    rope[:, 1, CTX:] = sinT[:, L0:L0 + NL]
    gq, gk = inp["g_q"][0], inp["g_k"][0]
    perm = (np.arange(128) + 64) % 128
    gqk = np.stack([gq, gq[perm], gk, gk[perm]], 1).astype(np.float32)
    rperm = np.zeros((128, 128), np.float32)
    rperm[perm, np.arange(128)] = 1.0
    emask = np.zeros((128, 2), np.float32)
    emask[:, 0] = 1.0 if L0 > 0 else 0.0
    emask[:, 1] = 1.0 if L0 + NL < SEQ else 0.0
    xT = np.ascontiguousarray(inp["x"][bi].T)
    return dict(
        MctxT=np.ascontiguousarray(Mfull[:, 0:CTX]), MlatT=halo_cols(Mfull[:, CTX:], L0),
        ctxT=np.ascontiguousarray(inp["ctx"][bi].T), xlatT=halo_cols(xT, L0), cT=mod_inputs(inp, bi),
        wmod0=inp["w_mod"][0], bmod0T=vecT(inp["b_mod"][0], 48), wmod1=inp["w_mod"][1], bmod1T=vecT(inp["b_mod"][1], 48),
        gffn0T=vecT(inp["g_norm_ffn"][0]), gmix1T=vecT(inp["g_norm_mix"][1]),
        wout=inp["w_even_out"][0], wup=inp["w_ffn_up"][0], wcT=conv_pack(inp, 0), wdn=inp["w_ffn_down"][0],
        emask=emask, wqkv=inp["w_qkv"][0], gqk=gqk, ropeT=rope, c_rperm=rperm.astype(NPBF), c_ones=cA["ones"])


NKB = NTOK // 128


def attention(b, qT_d, kT_d, v_d, att_d, qblocks, hooks=None, qsel=None):
    KT = b.sb([128, 2, NTOK], BF16, "KT")
    b.load(KT, KT[:], kT_d)
    V = b.sb([128, NKB, 256], BF16, "V")
    b.load(V, V[:], v_d.rearrange("(kb p) f -> p kb f", p=128))
    ones32 = b.sb([128, 128], F32, "ones32")
    b.memset(ones32, ones32[:], 1.0)
    Qr = b.rot(3, [128, 8, 512], BF16)
    Qbr = b.rot(2, [128, 8, 512], BF16) if qsel is not None else None
    pTr = b.rot(6, [128, 512], BF16)
    rlr = b.rot(2, [128, 512], F32)
    aor = b.rot(2, [128, 8, 512], BF16)
    accDr = b.rot(2, [128, 512], F32)
    accPr = b.rot(2, [128, 512], F32)
    attv = att_d.rearrange("(h p) t -> p h t", p=128)
    ROLE = ("pe", "pe", "pe", "dve", "dve", "dve", "pool", "pool")

    Qt = {}

    def qdma(j):
        q0_, Tq_ = qblocks[j]
        Q_ = Qr.next()
        Qb_ = None
        if qsel is None:
            b.load(Q_, Q_[:, :, 0:Tq_], qT_d[:, :, q0_:q0_ + Tq_])
        elif Tq_ == 512:
            Qb_ = Qbr.next()
            b.load(Q_, Q_[:, :, 0:512], qT_d[:, :, q0_:q0_ + 512])
            b.load(Qb_, Qb_[:, :, 0:512], qT_d[:, :, NL + q0_:NL + q0_ + 512])
        else:
            assert q0_ == NL and Tq_ == 2
            b.load(Q_, Q_[:, :, 0:2], qT_d[:, :, NL - 1:NL + 1], slow=True)
        Qt[j] = (Q_, Qb_)

    def qblend(j):
        if qsel is None:
            return
        q0_, Tq_ = qblocks[j]
        Q_, Qb_ = Qt[j]
        for r in range(8):
            if Tq_ == 512:
                b.ts(Q_[:, r, 0:512], Q_[:, r, 0:512], qsel[:, 0:1], None, ALU.mult, None, [qsel.k], [Q_.k])
                b.stt(Q_[:, r, 0:512], Qb_[:, r, 0:512], qsel[:, 1:2], Q_[:, r, 0:512], ALU.mult, ALU.add, [Qb_.k, qsel.k], [Q_.k])
            else:
                b.ts(Q_[:, r, 0:1], Q_[:, r, 0:1], qsel[:, 1:2], None, ALU.mult, None, [qsel.k], [Q_.k])
                b.ts(Q_[:, r, 1:2], Q_[:, r, 1:2], qsel[:, 0:1], None, ALU.mult, None, [qsel.k], [Q_.k])

    qdma(0)
    if len(qblocks) > 1:
        qdma(1)
    qblend(0)
    if hooks:
        hooks[0][0]()
    for qi, (q0, Tq) in enumerate(qblocks):
        Q = Qt[qi][0]
        if qi + 2 < len(qblocks):
            qdma(qi + 2)
        if hooks and qi + 1 < len(hooks):
            hooks[qi + 1][0]()
        if qi + 1 < len(qblocks):
            qblend(qi + 1)
        if hooks and qi < len(hooks):
            hooks[qi][1]()
        ao = aor.next()
        for h in range(8):
            kvh = h // 4
            accO = b.accb.next()
            accL = b.accb.next()
            accD = accDr.next()
            accP = accPr.next()
            seen = {"pe": False, "dve": False, "pool": False}

            def smm(kb):
                bk = b.banks.next()
                b.mm(bk, bk[:, 0:Tq], KT[:, kvh, kb * 128:(kb + 1) * 128], Q[:, h, 0:Tq], True, True, [KT.k, Q.k])
                return bk
            LOOK = 3
            assert len(b.banks.bufs) >= LOOK + 1
            pend = [smm(k) for k in range(min(LOOK, NKB))]
            for kb in range(NKB):
                if kb + LOOK < NKB:
                    pend.append(smm(kb + LOOK))
                bankS = pend.pop(0)
                pT = pTr.next()
                b.act(pT[:, 0:Tq], bankS[:, 0:Tq], AF.Exp, [], [pT.k, bankS.k])
                b.mm(accO, accO[:, 0:Tq], V[:, kb, kvh * 128:(kvh + 1) * 128], pT[:, 0:Tq], kb == 0, kb == NKB - 1, [V.k, pT.k])
                role = ROLE[kb % 8]
                if role == "pe":
                    b.mm(accL, accL[:, 0:Tq], b.ones[:], pT[:, 0:Tq], not seen["pe"], False, [b.ones.k, pT.k])
                else:
                    acc, eng = (accD, "vector") if role == "dve" else (accP, "gpsimd")
                    if not seen[role]:
                        b.copy(acc[:, 0:Tq], pT[:, 0:Tq], [pT.k], [acc.k], eng=eng)
                    else:
                        b.tt(acc[:, 0:Tq], acc[:, 0:Tq], pT[:, 0:Tq], ALU.add, [pT.k], [acc.k], eng=eng)
                seen[role] = True
            b.mm(accL, accL[:, 0:Tq], ones32[:], accD[:, 0:Tq], False, False, [ones32.k, accD.k])
            b.mm(accL, accL[:, 0:Tq], ones32[:], accP[:, 0:Tq], False, True, [ones32.k, accP.k])
            rl = rlr.next()
            b.recip(rl[:, 0:Tq], accL[:, 0:Tq], [], [rl.k, accL.k])
            b.tt(ao[:, h, 0:Tq], accO[:, 0:Tq], rl[:, 0:Tq], ALU.mult, [rl.k], [ao.k, accO.k])
        b.store(attv[:, :, q0:q0 + Tq], ao, ao[:, :, 0:Tq])


def build_C():
    b = B(nacc=4)
    nc = b.nc
    qT = b.din("qT", [128, 8, NLS], BF16)
    kT = b.din("kT", [128, 2, NTOK], BF16)
    vtm = b.din("vtm", [NTOK, 256], BF16)
    h1T = b.din("h1T", [D, NLS])
    cT = b.din("cT", [128, KC, 2])
    wmod1 = b.din("wmod1", [D, 6 * D])
    bmod1T = b.din("bmod1T", [128, 48])
    gffn1T = b.din("gffn1T", [128, KC])
    gfinT = b.din("gfinT", [128, KC])
    wout = b.din("wout", [D, D])
    wup = b.din("wup", [D, 2 * DFF])
    wcT = b.din("wcT", [128, 44, 4])
    wdn = b.din("wdn", [DFF, D])
    emask_d = b.din("emask", [128, 2])
    c_ones = b.din("c_ones", [128, 128], BF16)
    outT = b.dout("outT", [D, NL])
    att_d = nc.dram_tensor("att_d", [D, NLS], BF16).ap()
    hid_d = nc.dram_tensor("hid_d", [22, 128, NL], BF16).ap()
    hmid_d = nc.dram_tensor("hmid_d", [D, NL], F32).ap()

    b.consts(c_ones)
    mod1, (gscF1,) = b.adaln(cT, wmod1, bmod1T, [(gffn1T, 4)])
    emask = b.sb([128, 2], F32)
    b.load(emask, emask[:], emask_d)
    gfin = b.sb([128, KC, 1], F32)
    b.load(gfin, gfin[:, :, 0], gfinT)

    b.push_scope()
    attention(b, qT, kT, vtm, att_d, [(i * 512, 512) for i in range(NL // 512)] + [(NL, 2)])
    b.pop_scope()

    b.push_scope()
    Wout, Wup, wc = load_ffn_weights_p1(b, wout, wup, wcT)
    ffn_p1(b, [(att_d, h1T, NL, 0, True, 0)], Wout, Wup, wc, mod1, 2, gscF1, 3, emask, hid_d, hmid_d)
    b.pop_scope()

    b.push_scope()
    Wdn = b.sb([128, 22, D], BF16, "Wdn")
    b.load(Wdn, Wdn[:], wdn.rearrange("(j p) f -> p j f", p=128), queue="gpsimd")
    hidr = b.rot(2, [128, 22, 512], BF16)
    hmr = b.rot(2, [128, KC, 512], F32)
    sqr = b.rot(1, [128, KC, 512], BF16)
    rstdr = b.rot(2, [128, 512], F32)
    tmpr = b.rot(1, [128, KC, 512], F32)
    outr = b.rot(2, [128, KC, 512], F32)
    pre = ffn_p2_load(b, hid_d, hmid_d, 0, 512, hidr, hmr)
    for i in range(NL // 512):
        t0 = i * 512
        T = 512
        cur = pre
        if i + 1 < NL // 512:
            pre = ffn_p2_load(b, hid_d, hmid_d, t0 + 512, 512, hidr, hmr)
        hm = ffn_p2_block(b, Wdn, mod1, 5, 0, hid_d, hmid_d, t0, T, hidr, hmr, pre=cur)
        o = outr.next()
        b.norm_mod(hm, T, gfin, None, None, 0, o, sqr.next(), rstdr.next(), tmpr.next())
        b.store(outT.rearrange("(kc p) t -> p kc t", p=128)[:, :, t0:t0 + T], o, o[:, :, 0:T])
    b.pop_scope()
    b.P.finish()
    return nc


def prep_C(inp, bi, th, outB_pair, cA):
    L0 = th * NL
    h1full = np.concatenate([outB_pair[0]["h1T"], outB_pair[1]["h1T"]], 1)
    qfull = np.concatenate([outB_pair[0]["qT"], outB_pair[1]["qT"]], 2)
    q = halo_cols(qfull.reshape(1024, SEQ), L0).reshape(128, 8, NLS)
    kall = np.concatenate([outB_pair[0]["kT"][:, :, 0:CTX], outB_pair[0]["kT"][:, :, CTX:], outB_pair[1]["kT"][:, :, CTX:]], 2)
    vall = np.concatenate([outB_pair[0]["vtm"][0:CTX], outB_pair[0]["vtm"][CTX:], outB_pair[1]["vtm"][CTX:]], 0)
    emask = np.zeros((128, 2), np.float32)
    emask[:, 0] = 1.0 if L0 > 0 else 0.0
    emask[:, 1] = 1.0 if L0 + NL < SEQ else 0.0
    return dict(
        qT=np.ascontiguousarray(q), kT=np.ascontiguousarray(kall), vtm=np.ascontiguousarray(vall), h1T=halo_cols(h1full, L0),
        cT=mod_inputs(inp, bi), wmod1=inp["w_mod"][1], bmod1T=vecT(inp["b_mod"][1], 48),
        gffn1T=vecT(inp["g_norm_ffn"][1]), gfinT=vecT(inp["g_norm_final"]),
        wout=inp["w_att_out"][0], wup=inp["w_ffn_up"][1], wcT=conv_pack(inp, 1), wdn=inp["w_ffn_down"][1],
        emask=emask, c_ones=cA["ones"])


def stage_B_full(b, mix_d, xT, ctxT, mod0, gscF0, mod1, gsc1, wout, wup, wcT, wdn, wqkv, gqk_d, ropeT, c_rperm,
                 h1_d, q_d, k_d, v_d, hid_d, hmid_d):
    zmask = b.sb([128, 2], F32)
    b.memset(zmask, zmask[:], 0.0)
    b.push_scope()
    Wout, Wup, wc = load_ffn_weights_p1(b, wout, wup, wcT)
    ffn_p1(b, [(mix_d[:, 0:CTX], ctxT, CTX, 1, False, 0), (mix_d[:, CTX:NTOK], xT, SEQ, 0, False, CTX)],
           Wout, Wup, wc, mod0, 2, gscF0, 3, zmask, hid_d, hmid_d)
    b.pop_scope()
    b.push_scope()
    Wdn = b.sb([128, 22, D], BF16, "Wdn")
    b.load(Wdn, Wdn[:], wdn.rearrange("(j p) f -> p j f", p=128), queue="gpsimd")
    Wqkv = b.sb([128, KC, 1536], BF16, "Wqkv")
    b.load(Wqkv, Wqkv[:], wqkv.rearrange("(kc p) f -> p kc f", p=128), queue="gpsimd")
    gqk = b.sb([128, 4], F32)
    b.load(gqk, gqk[:], gqk_d)
    roper = b.rot(1, [128, 2, 512], F32)
    rperm = b.sb([128, 128], BF16)
    b.load(rperm, rperm[:], c_rperm)
    epsq = b.sb([128, 1], F32)
    b.memset(epsq, epsq[:], 128 * EPS)
    hidr = b.rot(2, [128, 22, 512], BF16)
    hmr = b.rot(2, [128, KC, 512], F32)
    sqr = b.rot(1, [128, KC, 512], BF16)
    rstdr = b.rot(2, [128, 512], F32)
    a1r = b.rot(2, [128, KC, 512], BF16)
    qbr = b.rot(2, [128, 512], BF16)
    sqbr = b.rot(2, [128, 512], BF16)
    rqr = b.rot(2, [128, 512], F32)
    t1r = b.rot(2, [128, 512], F32)
    t2r = b.rot(2, [128, 512], F32)
    qsr = b.rot(1, [128, 10, 512], BF16)
    vstr = b.rot(3, [128, 256], BF16)
    blocks = [(1, 0, CTX, 0)] + [(0, i * 512, 512, CTX) for i in range(SEQ // 512)]

    def p2_load(i):
        col, t0, T, cbase = blocks[i]
        return ffn_p2_load(b, hid_d, hmid_d, cbase + t0, T, hidr, hmr)

    def down_a(i, pre):
        col, t0, T, cbase = blocks[i]
        hm = ffn_p2_block(b, Wdn, mod0, 5, col, hid_d, hmid_d, cbase + t0, T, hidr, hmr, pre=pre)
        if col == 0:
            b.store(h1_d.rearrange("(kc p) t -> p kc t", p=128)[:, :, t0:t0 + T], hm, hm[:, :, 0:T])
        sq = sqr.next()
        b.norm_a(hm, T, sq)
        return dict(i=i, hm=hm, sq=sq)

    def down_b(c):
        col, t0, T, cbase = blocks[c["i"]]
        a1 = a1r.next()
        b.norm_b(c["hm"], T, gsc1, mod1, 0, col, a1, c["sq"], rstdr.next(), c["sq"])
        c["a1"] = a1

    loads = {0: p2_load(0)}
    if len(blocks) > 1:
        loads[1] = p2_load(1)
    cur = down_a(0, loads.pop(0))
    down_b(cur)
    for bi_, (col, t0, T, cbase) in enumerate(blocks):
        is_ctx = col == 1
        c0 = cbase + t0
        a1 = cur["a1"]
        if bi_ + 2 < len(blocks):
            loads[bi_ + 2] = p2_load(bi_ + 2)
        nxt_c = None
        qs = qsr.next()
        rope = roper.next()
        b.load(rope, rope[:, :, 0:T], ropeT[:, :, c0:c0 + T])
        hds = list(range(8, 10) if is_ctx else range(10))
        ia, ib = (0, 1) if is_ctx else (4, 8)

        def hproj(hd_):
            bk = b.banks.next()
            for kc in range(KC):
                b.mm(bk, bk[:, 0:T], Wqkv[:, kc, hd_ * 128:(hd_ + 1) * 128], a1[:, kc, 0:T], kc == 0, kc == KC - 1,
                     [Wqkv.k, a1.k])
            return bk
        assert len(b.banks.bufs) >= 6
        bank_n = hproj(hds[0])
        for hi_, hd in enumerate(hds):
            isq = hd < 8
            bank = bank_n
            qb = qbr.next()
            b.act(qb[:, 0:T], bank[:, 0:T], AF.Copy, [], [qb.k, bank.k])
            sqb = sqbr.next()
            b.act(sqb[:, 0:T], bank[:, 0:T], AF.Square, [], [sqb.k, bank.k])
            if hi_ == ia and bi_ + 1 < len(blocks):
                nxt_c = down_a(bi_ + 1, loads.pop(bi_ + 1))
            if hi_ == ib and nxt_c is not None:
                down_b(nxt_c)
            if hi_ + 1 < len(hds):
                bank_n = hproj(hds[hi_ + 1])
            bankS = b.banks.next()
            b.mm(bankS, bankS[:, 0:T], b.ones[:], sqb[:, 0:T], True, True, [b.ones.k, sqb.k])
            bankP = b.banks.next()
            b.mm(bankP, bankP[:, 0:T], rperm[:], qb[:, 0:T], True, True, [rperm.k, qb.k])
            rq = rqr.next()
            if isq:
                b.act(rq[:, 0:T], bankS[:, 0:T], AF.Ln, [epsq.k], [rq.k, bankS.k], scale=1.0, bias=epsq[:])
            else:
                b.act(rq[:, 0:T], bankS[:, 0:T], AF.Ln, [b.epsb.k], [rq.k, bankS.k], scale=1.0 / 128, bias=b.epsb[:])
            b.act(rq[:, 0:T], rq[:, 0:T], AF.Exp, [], [rq.k], scale=-0.5)
            gi = 0 if isq else 2
            t1 = t1r.next()
            b.stt(t1[:, 0:T], qb[:, 0:T], gqk[:, gi:gi + 1], rope[:, 0, 0:T], ALU.mult, ALU.mult, [qb.k, gqk.k, rope.k], [t1.k])
            t2 = t2r.next()
            b.stt(t2[:, 0:T], bankP[:, 0:T], gqk[:, gi + 1:gi + 2], rope[:, 1, 0:T], ALU.mult, ALU.mult, [gqk.k, rope.k],
                  [t2.k, bankP.k])
            b.tt(t1[:, 0:T], t1[:, 0:T], t2[:, 0:T], ALU.add, [t2.k], [t1.k])
            b.tt(qs[:, hd, 0:T], t1[:, 0:T], rq[:, 0:T], ALU.mult, [t1.k, rq.k], [qs.k])
        if not is_ctx:
            b.store(q_d[:, :, t0:t0 + T], qs, qs[:, 0:8, 0:T])
        b.store(k_d[:, :, c0:c0 + T], qs, qs[:, 8:10, 0:T])
        for s in range(T // 128):
            bank = b.banks.next()
            for kc in range(KC):
                b.mm(bank, bank[:, 0:256], a1[:, kc, s * 128:(s + 1) * 128], Wqkv[:, kc, 1280:1536], kc == 0, kc == KC - 1,
                     [a1.k, Wqkv.k])
            vst = vstr.next()
            b.copy(vst[:], bank[:, 0:256], [], [vst.k, bank.k])
            b.store(v_d[c0 + s * 128:c0 + (s + 1) * 128, :], vst, vst[:])
        cur = nxt_c
    b.pop_scope()


def select_half(b, msel, src_v, dst_v, nrow, dt):
    ar = b.rot(2, [128, nrow, 512], dt)
    br = b.rot(2, [128, nrow, 512], dt)
    for i in range(NL // 512):
        a = ar.next()
        c = br.next()
        b.load(a, a[:], src_v[:, :, i * 512:(i + 1) * 512])
        b.load(c, c[:], src_v[:, :, NL + i * 512:NL + (i + 1) * 512])
        for r in range(nrow):
            b.ts(a[:, r, :], a[:, r, :], msel[:, 0:1], None, ALU.mult, None, [msel.k], [a.k])
            b.stt(a[:, r, :], c[:, r, :], msel[:, 1:2], a[:, r, :], ALU.mult, ALU.add, [c.k, msel.k], [a.k])
        b.store(dst_v[:, :, i * 512:(i + 1) * 512], a, a[:])
    h = b.sb([128, nrow, 2], dt)
    b.load(h, h[:], src_v[:, :, NL - 1:NL + 1], slow=True)
    for r in range(nrow):
        b.ts(h[:, r, 0:1], h[:, r, 0:1], msel[:, 1:2], None, ALU.mult, None, [msel.k], [h.k])
        b.ts(h[:, r, 1:2], h[:, r, 1:2], msel[:, 0:1], None, ALU.mult, None, [msel.k], [h.k])
    b.store(dst_v[:, :, NL:NL + 2], h, h[:])


def select_hooks(b, msel, src_v, dst_v, nrow, dt):
    ar = b.rot(2, [128, nrow, 512], dt)
    br = b.rot(2, [128, nrow, 512], dt)
    h = b.sb([128, nrow, 2], dt)
    held = {}

    def blk(i):
        def load():
            a = ar.next()
            c = br.next()
            b.load(a, a[:], src_v[:, :, i * 512:(i + 1) * 512])
            b.load(c, c[:], src_v[:, :, NL + i * 512:NL + (i + 1) * 512])
            held[i] = (a, c)

        def run():
            a, c = held.pop(i)
            for r in range(nrow):
                b.ts(a[:, r, :], a[:, r, :], msel[:, 0:1], None, ALU.mult, None, [msel.k], [a.k])
                b.stt(a[:, r, :], c[:, r, :], msel[:, 1:2], a[:, r, :], ALU.mult, ALU.add, [c.k, msel.k], [a.k])
            b.store(dst_v[:, :, i * 512:(i + 1) * 512], a, a[:])
        return load, run

    def halo_load():
        b.load(h, h[:], src_v[:, :, NL - 1:NL + 1], slow=True)

    def halo_run():
        for r in range(nrow):
            b.ts(h[:, r, 0:1], h[:, r, 0:1], msel[:, 1:2], None, ALU.mult, None, [msel.k], [h.k])
            b.ts(h[:, r, 1:2], h[:, r, 1:2], msel[:, 0:1], None, ALU.mult, None, [msel.k], [h.k])
        b.store(dst_v[:, :, NL:NL + 2], h, h[:])
    return [blk(i) for i in range(NL // 512)] + [(halo_load, halo_run)]


def build_fused():
    b = B(nacc=4)
    nc = b.nc
    xT = b.din("xT", [D, SEQ])
    ctxT = b.din("ctxT", [D, CTX])
    cT = b.din("cT", [128, KC, 2])
    wmod0 = b.din("wmod0", [D, 6 * D])
    bmod0T = b.din("bmod0T", [128, 48])
    wmod1 = b.din("wmod1", [D, 6 * D])
    bmod1T = b.din("bmod1T", [128, 48])
    gmix0T = b.din("gmix0T", [128, KC])
    gffn0T = b.din("gffn0T", [128, KC])
    gmix1T = b.din("gmix1T", [128, KC])
    gffn1T = b.din("gffn1T", [128, KC])
    gfinT = b.din("gfinT", [128, KC])
    hpw = [dict(winfm=b.din(f"winfm{hp}", [D, 896]), wintm=b.din(f"wintm{hp}", [D, 384]), wg=b.din(f"wg{hp}", [128, 2, 128]),
                bgp=b.din(f"bgp{hp}", [128, 2, 128]), ggla=b.din(f"ggla{hp}", [128, 2])) for hp in range(2)]
    c_ones = b.din("c_ones", [128, 128], BF16)
    cst = (b.din("c_tri", [128, 4, 128]), b.din("c_mask4", [128, 4, 128], BF16), b.din("c_csg", [128, 256], BF16),
           b.din("c_tt", [128, 64, 384], BF16), b.din("c_ident", [128, 128], BF16), b.din("c_L", [128, 64], BF16),
           b.din("c_cs256", [128, 2, 2, 256], BF16))
    wout0 = b.din("wout0", [D, D])
    wup0 = b.din("wup0", [D, 2 * DFF])
    wcT0 = b.din("wcT0", [128, 44, 4])
    wdn0 = b.din("wdn0", [DFF, D])
    wqkv = b.din("wqkv", [D, 1536])
    gqk_d = b.din("gqk", [128, 4])
    ropeT = b.din("ropeT", [128, 2, NTOK])
    c_rperm = b.din("c_rperm", [128, 128], BF16)
    msel_d = b.din("msel", [128, 2])
    emask_d = b.din("emask", [128, 2])
    wout1 = b.din("wout1", [D, D])
    wup1 = b.din("wup1", [D, 2 * DFF])
    wcT1 = b.din("wcT1", [128, 44, 4])
    wdn1 = b.din("wdn1", [DFF, D])
    outT = b.dout("outT", [D, NL])

    mix_d = nc.dram_tensor("mix_d", [D, NTOK], BF16).ap()
    hid_d = nc.dram_tensor("hid_d", [22, 128, NTOK], BF16).ap()
    hmid_d = nc.dram_tensor("hmid_d", [D, NTOK], F32).ap()
    h1_d = nc.dram_tensor("h1_d", [D, SEQ], F32).ap()
    q_d = nc.dram_tensor("q_d", [128, 8, SEQ], BF16).ap()
    k_d = nc.dram_tensor("k_d", [128, 2, NTOK], BF16).ap()
    v_d = nc.dram_tensor("v_d", [NTOK, 256], BF16).ap()
    h1s_d = nc.dram_tensor("h1s_d", [D, NLS], F32).ap()
    qs_d = nc.dram_tensor("qs_d", [128, 8, NLS], BF16).ap()
    att_d = nc.dram_tensor("att_d", [D, NLS], BF16).ap()

    b.consts(c_ones)
    mod0, (gsc0, gscF0) = b.adaln(cT, wmod0, bmod0T, [(gmix0T, 1), (gffn0T, 4)])
    mod1, (gsc1, gscF1) = b.adaln(cT, wmod1, bmod1T, [(gmix1T, 1), (gffn1T, 4)])
    msel = b.sb([128, 2], F32)
    b.load(msel, msel[:], msel_d)
    emask = b.sb([128, 2], F32)
    b.load(emask, emask[:], emask_d)
    gfin = b.sb([128, KC, 1], F32)
    b.load(gfin, gfin[:, :, 0], gfinT)

    for hp in range(2):
        w = hpw[hp]
        stage_A(b, xT, ctxT, w["winfm"], w["wintm"], w["wg"], w["bgp"], w["ggla"], cst, mix_d, 256 * hp, 512 + 256 * hp, mod0, gsc0)
    b.set_acc(2)
    stage_B_full(b, mix_d, xT, ctxT, mod0, gscF0, mod1, gsc1, wout0, wup0, wcT0, wdn0, wqkv, gqk_d, ropeT, c_rperm,
                 h1_d, q_d, k_d, v_d, hid_d, hmid_d)
    b.set_acc(4)
    b.push_scope()
    h1_hooks = select_hooks(b, msel, h1_d.rearrange("(kc p) t -> p kc t", p=128), h1s_d.rearrange("(kc p) t -> p kc t", p=128), KC, F32)
    attention(b, q_d, k_d, v_d, att_d, [(i * 512, 512) for i in range(NL // 512)] + [(NL, 2)], hooks=h1_hooks, qsel=msel)
    b.pop_scope()
    b.set_acc(2)
    b.push_scope()
    Wout, Wup, wc = load_ffn_weights_p1(b, wout1, wup1, wcT1)
    ffn_p1(b, [(att_d, h1s_d, NL, 0, True, 0)], Wout, Wup, wc, mod1, 2, gscF1, 3, emask, hid_d, hmid_d)
    b.pop_scope()
    b.push_scope()
    Wdn = b.sb([128, 22, D], BF16, "Wdn")
    b.load(Wdn, Wdn[:], wdn1.rearrange("(j p) f -> p j f", p=128), queue="gpsimd")
    hidr = b.rot(2, [128, 22, 512], BF16)
    hmr = b.rot(2, [128, KC, 512], F32)
    sqr = b.rot(1, [128, KC, 512], BF16)
    rstdr = b.rot(2, [128, 512], F32)
    tmpr = b.rot(1, [128, KC, 512], F32)
    outr = b.rot(2, [128, KC, 512], F32)
    pre = ffn_p2_load(b, hid_d, hmid_d, 0, 512, hidr, hmr)
    for i in range(NL // 512):
        t0 = i * 512
        cur = pre
        if i + 1 < NL // 512:
            pre = ffn_p2_load(b, hid_d, hmid_d, t0 + 512, 512, hidr, hmr)
        hm = ffn_p2_block(b, Wdn, mod1, 5, 0, hid_d, hmid_d, t0, 512, hidr, hmr, pre=cur)
        o = outr.next()
        b.norm_mod(hm, 512, gfin, None, None, 0, o, sqr.next(), rstdr.next(), tmpr.next())
        b.store(outT.rearrange("(kc p) t -> p kc t", p=128)[:, :, t0:t0 + 512], o, o[:, :, 0:512])
    b.pop_scope()
    b.P.finish()
    return nc


def prep_fused(inp, bi, th, cA, rt):
    L0 = th * NL
    m = {}
    for hp in range(2):
        a = prep_A(inp, bi, hp, cA)
        for k in ("winfm", "wintm", "wg", "bgp"):
            m[f"{k}{hp}"] = a[k]
        m[f"ggla{hp}"] = a["gglaT"]
        if hp == 0:
            for k in ("xT", "ctxT", "cT", "c_ones", "c_tri", "c_mask4", "c_csg", "c_tt", "c_ident", "c_L", "c_cs256"):
                m[k] = a[k]
    cosT, sinT = rt
    rope = np.zeros((128, 2, NTOK), np.float32)
    rope[:, 0, 0:CTX] = 1.0
    rope[:, 0, CTX:] = cosT
    rope[:, 1, CTX:] = sinT
    gq, gk = inp["g_q"][0], inp["g_k"][0]
    perm = (np.arange(128) + 64) % 128
    rperm = np.zeros((128, 128), np.float32)
    rperm[perm, np.arange(128)] = 1.0
    msel = np.zeros((128, 2), np.float32)
    msel[:, th] = 1.0
    emask = np.zeros((128, 2), np.float32)
    emask[:, 0] = 1.0 if L0 > 0 else 0.0
    emask[:, 1] = 1.0 if L0 + NL < SEQ else 0.0
    m.update(
        wmod0=inp["w_mod"][0], bmod0T=vecT(inp["b_mod"][0], 48), wmod1=inp["w_mod"][1], bmod1T=vecT(inp["b_mod"][1], 48),
        gmix0T=vecT(inp["g_norm_mix"][0]), gffn0T=vecT(inp["g_norm_ffn"][0]), gmix1T=vecT(inp["g_norm_mix"][1]),
        gffn1T=vecT(inp["g_norm_ffn"][1]), gfinT=vecT(inp["g_norm_final"]),
        wout0=inp["w_even_out"][0], wup0=inp["w_ffn_up"][0], wcT0=conv_pack(inp, 0), wdn0=inp["w_ffn_down"][0],
        wqkv=inp["w_qkv"][0], gqk=np.stack([gq, gq[perm], gk, gk[perm]], 1).astype(np.float32), ropeT=rope,
        c_rperm=rperm.astype(NPBF), msel=msel, emask=emask,
        wout1=inp["w_att_out"][0], wup1=inp["w_ffn_up"][1], wcT1=conv_pack(inp, 1), wdn1=inp["w_ffn_down"][1])
    return m


def kernel(**inputs):
    inp = {k: np.asarray(v) for k, v in inputs.items()}
    cA = consts_A()
    rt = rope_tables()
    cores = list(range(8))
    nc = build_fused()
    res = run_bass_kernel_spmd(nc, [prep_fused(inp, c // 2, c % 2, cA, rt) for c in cores], core_ids=cores)
    out = np.zeros((4, SEQ, D), np.float32)
    for c in cores:
        bi, th = c // 2, c % 2
        out[bi, th * NL:(th + 1) * NL, :] = np.asarray(res.results[c]["outT"]).T
    return out
```
